# Optimizing a Trainium2 kernel written in Bass

```python
import jax, jax.numpy as jnp
from jax import lax
import numpy as np

D_MODEL = 1024
BATCH = 2
SEQ = 8192
DEPTH = 1
DEC_BATCH = 128
DEC_SEQ = 4
PAST_LEN = 2048
PAGE_SIZE = 128

D_MIX = D_MODEL
D_ATT = D_MIX // 2
H_ATT = 8
HD_ATT = D_ATT // H_ATT
D_SSM = D_MIX - D_ATT
P_SSM = 64
H_SSM = D_SSM // P_SSM
N_STATE = 128
N_BC_GROUPS = 2
HEADS_PER_BC_GROUP = H_SSM // N_BC_GROUPS
CONV_WIDTH = 4
CONV_CH = D_SSM + 2 * N_BC_GROUPS * N_STATE
SSD_CHUNK = 128
Q_BLOCK = 128
N_MEM = 256
H_X = 4
HD_X = D_MODEL // H_X
D_FF = -(-(8 * D_MODEL) // (3 * 256)) * 256
IN_COLS = 3 * D_ATT + H_ATT + D_SSM + CONV_CH + H_SSM
EPS = 1e-6
ATT_SCALE = HD_ATT ** -0.5
X_SCALE = HD_X ** -0.5

kernel_name = 'hymba_fox_ssd_decode_step'


def rms_norm(x, g):
    xf = x.astype(jnp.float32)
    xf = xf * lax.rsqrt(jnp.mean(xf * xf, axis=-1, keepdims=True) + EPS)
    return (xf * g.astype(jnp.float32)).astype(x.dtype)


def in_projection(h, w_in, b_forget):
    b, l, _ = h.shape
    proj = jnp.einsum('bld,dc->blc', h, w_in)
    cuts = [D_ATT, 2 * D_ATT, 3 * D_ATT, 3 * D_ATT + H_ATT,
            3 * D_ATT + H_ATT + D_SSM, 3 * D_ATT + H_ATT + D_SSM + CONV_CH]
    q, k, v, f_raw, z, xbc, dt_raw = jnp.split(proj, cuts, axis=-1)
    q = q.reshape(b, l, H_ATT, HD_ATT)
    k = k.reshape(b, l, H_ATT, HD_ATT)
    v = v.reshape(b, l, H_ATT, HD_ATT)
    logf = jax.nn.log_sigmoid((f_raw + b_forget).astype(jnp.float32))
    return q, k, v, logf, z, xbc, dt_raw


def fox_attend(q, c_q, k, v, c_k, mask):
    s = jnp.einsum('bqhd,bkhd->bhqk', q, k).astype(jnp.float32) * ATT_SCALE
    bias = jnp.swapaxes(c_q, 1, 2)[:, :, :, None] - jnp.swapaxes(c_k, 1, 2)[:, :, None, :]
    s = jnp.where(mask, s + bias, -jnp.inf)
    p = jax.nn.softmax(s, axis=-1)
    return jnp.einsum('bhqk,bkhd->bqhd', p.astype(v.dtype), v)


def fox_prompt(q, k, v, logf):
    b, l, h, d = q.shape
    c = jnp.cumsum(logf, axis=1)
    kpos = jnp.arange(l)

    def block(i):
        start = i * Q_BLOCK
        q_blk = lax.dynamic_slice_in_dim(q, start, Q_BLOCK, axis=1)
        c_blk = lax.dynamic_slice_in_dim(c, start, Q_BLOCK, axis=1)
        qpos = start + jnp.arange(Q_BLOCK)
        mask = qpos[:, None] >= kpos[None, :]
        return fox_attend(q_blk, c_blk, k, v, c, mask)

    out = lax.map(block, jnp.arange(l // Q_BLOCK))
    return jnp.moveaxis(out, 0, 1).reshape(b, l, h * d)


def fox_sample(q, k_new, v_new, logf_new, cache_k, cache_v, cache_logf, page_table):
    db, l = q.shape[:2]
    past = page_table.shape[1] * PAGE_SIZE
    k_past = cache_k[page_table].reshape(db, past, H_ATT, HD_ATT)
    v_past = cache_v[page_table].reshape(db, past, H_ATT, HD_ATT)
    logf_past = cache_logf[page_table].reshape(db, past, H_ATT).astype(jnp.float32)
    c_past = jnp.cumsum(logf_past, axis=1)
    c_new = c_past[:, -1:] + jnp.cumsum(logf_new, axis=1)
    k_all = jnp.concatenate([k_past.astype(k_new.dtype), k_new], axis=1)
    v_all = jnp.concatenate([v_past.astype(v_new.dtype), v_new], axis=1)
    c_all = jnp.concatenate([c_past, c_new], axis=1)
    qpos = past + jnp.arange(l)
    kpos = jnp.arange(past + l)
    mask = qpos[:, None] >= kpos[None, :]
    return fox_attend(q, c_new, k_all, v_all, c_all, mask).reshape(db, l, D_ATT)


def ssd_scan(xdt, a, bmat, cmat, h0):
    b, l, h, p = xdt.shape
    n = bmat.shape[-1]
    q = min(SSD_CHUNK, l)
    pad = (-l) % q
    if pad:
        xdt = jnp.pad(xdt, ((0, 0), (0, pad), (0, 0), (0, 0)))
        bmat = jnp.pad(bmat, ((0, 0), (0, pad), (0, 0), (0, 0)))
        cmat = jnp.pad(cmat, ((0, 0), (0, pad), (0, 0), (0, 0)))
        a = jnp.pad(a, ((0, 0), (0, pad), (0, 0)))
    nc = (l + pad) // q
    xdt = xdt.reshape(b, nc, q, h, p)
    bmat = bmat.reshape(b, nc, q, h, n)
    cmat = cmat.reshape(b, nc, q, h, n)
    a_cs = jnp.cumsum(a.reshape(b, nc, q, h), axis=2)
    causal = jnp.tril(jnp.ones((q, q), dtype=bool))
    seg = a_cs[:, :, :, None, :] - a_cs[:, :, None, :, :]
    decay_ts = jnp.exp(jnp.where(causal[None, None, :, :, None], seg, -jnp.inf))
    g = jnp.einsum('bcthn,bcshn->bctsh', cmat, bmat) * decay_ts
    y_diag = jnp.einsum('bctsh,bcshp->bcthp', g, xdt)
    decay_to_end = jnp.exp(a_cs[:, :, -1:, :] - a_cs)
    s_chunk = jnp.einsum('bcshn,bcsh,bcshp->bchpn', bmat, decay_to_end, xdt).astype(jnp.float32)
    chunk_decay = jnp.exp(a_cs[:, :, -1, :])

    def step(state, inp):
        s_c, d_c = inp
        return d_c[:, :, None, None] * state + s_c, state

    h_final, h_start = lax.scan(step, h0.astype(jnp.float32),
                                (jnp.moveaxis(s_chunk, 1, 0), jnp.moveaxis(chunk_decay, 1, 0)))
    h_start = jnp.moveaxis(h_start, 0, 1)
    y_off = jnp.einsum('bcthn,bchpn,bcth->bcthp', cmat, h_start, jnp.exp(a_cs))
    y = (y_diag + y_off).reshape(b, nc * q, h, p)[:, :l]
    return y, h_final


def ssd_branch(z, xbc, dt_raw, conv_buf, h0, conv_w, conv_b, dt_bias, a_log, d_skip, norm_g):
    b, l, _ = xbc.shape
    xpad = jnp.concatenate([conv_buf.astype(xbc.dtype), xbc], axis=1)
    acc = conv_b
    for tap in range(CONV_WIDTH):
        acc = acc + xpad[:, tap:tap + l] * conv_w[tap]
    new_buf = xpad[:, l:]
    act = jax.nn.silu(acc)
    xs = act[..., :D_SSM].reshape(b, l, H_SSM, P_SSM)
    bm = act[..., D_SSM:D_SSM + N_BC_GROUPS * N_STATE].reshape(b, l, N_BC_GROUPS, N_STATE)
    cm = act[..., D_SSM + N_BC_GROUPS * N_STATE:].reshape(b, l, N_BC_GROUPS, N_STATE)
    bm = jnp.repeat(bm, HEADS_PER_BC_GROUP, axis=2)
    cm = jnp.repeat(cm, HEADS_PER_BC_GROUP, axis=2)
    dt = jax.nn.softplus((dt_raw + dt_bias).astype(jnp.float32))
    a = -jnp.exp(a_log.astype(jnp.float32)) * dt
    y, h_new = ssd_scan(xs * dt[..., None], a, bm, cm, h0)
    y = y + d_skip[:, None] * xs
    y = y.reshape(b, l, D_SSM) * jax.nn.silu(z)
    return rms_norm(y, norm_g), new_buf, h_new


def mixer_sublayer(x, attend, conv_buf, h0, norm_g, w_in, b_forget, conv_w, conv_b,
                   dt_bias, a_log, d_skip, ssm_norm_g, w_out):
    h = rms_norm(x, norm_g)
    q, k, v, logf, z, xbc, dt_raw = in_projection(h, w_in, b_forget)
    att = attend(q, k, v, logf)
    ssm, conv_new, h_new = ssd_branch(z, xbc, dt_raw, conv_buf, h0, conv_w, conv_b,
                                      dt_bias, a_log, d_skip, ssm_norm_g)
    mixed = jnp.concatenate([att, ssm], axis=-1)
    x = x + jnp.einsum('blc,cd->bld', mixed, w_out).astype(x.dtype)
    return x, k, v, logf, conv_new, h_new


def memory_kv(mem, g, w_ck, w_cv):
    b, m, _ = mem.shape
    mn = rms_norm(mem, g)
    k = jnp.einsum('bmd,de->bme', mn, w_ck).reshape(b, m, H_X, HD_X)
    v = jnp.einsum('bmd,de->bme', mn, w_cv).reshape(b, m, H_X, HD_X)
    return k, v


def cross_sublayer(x, mem_k, mem_v, norm_g, w_cq, w_co):
    b, l, _ = x.shape
    h = rms_norm(x, norm_g)
    q = jnp.einsum('bld,de->ble', h, w_cq).reshape(b, l, H_X, HD_X)
    s = jnp.einsum('bqhd,bmhd->bhqm', q, mem_k.astype(q.dtype)).astype(jnp.float32) * X_SCALE
    p = jax.nn.softmax(s, axis=-1)
    o = jnp.einsum('bhqm,bmhd->bqhd', p.astype(q.dtype), mem_v.astype(q.dtype)).reshape(b, l, D_MODEL)
    return x + jnp.einsum('ble,ed->bld', o, w_co).astype(x.dtype)


def ffn_sublayer(x, norm_g, w_gate, w_up, w_down):
    h = rms_norm(x, norm_g)
    u = jax.nn.silu(jnp.einsum('bld,df->blf', h, w_gate)) * jnp.einsum('bld,df->blf', h, w_up)
    return x + jnp.einsum('blf,fd->bld', u, w_down).astype(x.dtype)


def setup_inputs(seed: int = 0) -> dict:
    key = jax.random.key(seed)
    keys = list(jax.random.split(key, 48))

    def nxt():
        return keys.pop()

    f32 = jnp.float32

    def normal(shape, scale=1.0):
        return jax.random.normal(nxt(), shape, f32) * scale

    def gain(shape):
        return 1.0 + 0.02 * normal(shape)

    n_pages = PAST_LEN // PAGE_SIZE
    n_used = DEC_BATCH * n_pages
    n_phys = n_used + max(n_used // 4, 1)

    x_prompt = normal((BATCH, SEQ, D_MODEL))
    x_sample = normal((DEC_BATCH, DEC_SEQ, D_MODEL))
    mem_prompt = normal((BATCH, N_MEM, D_MODEL))
    cache_k = normal((DEPTH, n_phys, PAGE_SIZE, H_ATT, HD_ATT))
    cache_v = normal((DEPTH, n_phys, PAGE_SIZE, H_ATT, HD_ATT))
    cache_logf = jax.nn.log_sigmoid(normal((DEPTH, n_phys, PAGE_SIZE, H_ATT)) + 2.0)
    page_table = jax.random.permutation(nxt(), n_phys)[:n_used].reshape(DEC_BATCH, n_pages).astype(jnp.int32)
    cache_mem_k = normal((DEPTH, DEC_BATCH, N_MEM, H_X, HD_X))
    cache_mem_v = normal((DEPTH, DEC_BATCH, N_MEM, H_X, HD_X))
    state_conv = normal((DEPTH, DEC_BATCH, CONV_WIDTH - 1, CONV_CH))
    state_ssm = normal((DEPTH, DEC_BATCH, H_SSM, P_SSM, N_STATE), 0.1)

    norm_mix_g = gain((DEPTH, D_MODEL))
    w_in = normal((DEPTH, D_MODEL, IN_COLS), D_MODEL ** -0.5)
    b_forget = jax.random.uniform(nxt(), (DEPTH, H_ATT), f32, 0.5, 3.0)
    conv_w = normal((DEPTH, CONV_WIDTH, CONV_CH), CONV_WIDTH ** -0.5)
    conv_b = normal((DEPTH, CONV_CH), 0.02)
    dt0 = jnp.exp(jax.random.uniform(nxt(), (DEPTH, H_SSM), f32, float(np.log(1e-3)), float(np.log(1e-1))))
    dt_bias = dt0 + jnp.log(-jnp.expm1(-dt0))
    a_log = jnp.log(jax.random.uniform(nxt(), (DEPTH, H_SSM), f32, 1.0, 16.0))
    d_skip = gain((DEPTH, H_SSM))
    ssm_norm_g = gain((DEPTH, D_SSM))
    w_out = normal((DEPTH, D_MIX, D_MODEL), D_MIX ** -0.5)
    norm_cross_g = gain((DEPTH, D_MODEL))
    norm_mem_g = gain((DEPTH, D_MODEL))
    w_cq = normal((DEPTH, D_MODEL, D_MODEL), D_MODEL ** -0.5)
    w_ck = normal((DEPTH, D_MODEL, D_MODEL), D_MODEL ** -0.5)
    w_cv = normal((DEPTH, D_MODEL, D_MODEL), D_MODEL ** -0.5)
    w_co = normal((DEPTH, D_MODEL, D_MODEL), D_MODEL ** -0.5)
    norm_ffn_g = gain((DEPTH, D_MODEL))
    w_gate = normal((DEPTH, D_MODEL, D_FF), D_MODEL ** -0.5)
    w_up = normal((DEPTH, D_MODEL, D_FF), D_MODEL ** -0.5)
    w_down = normal((DEPTH, D_FF, D_MODEL), D_FF ** -0.5)
    final_norm_g = gain((D_MODEL,))
    return {'x_prompt': x_prompt, 'x_sample': x_sample, 'mem_prompt': mem_prompt,
            'cache_k': cache_k, 'cache_v': cache_v, 'cache_logf': cache_logf,
            'page_table': page_table, 'cache_mem_k': cache_mem_k, 'cache_mem_v': cache_mem_v,
            'state_conv': state_conv, 'state_ssm': state_ssm,
            'norm_mix_g': norm_mix_g, 'w_in': w_in, 'b_forget': b_forget,
            'conv_w': conv_w, 'conv_b': conv_b, 'dt_bias': dt_bias, 'a_log': a_log,
            'd_skip': d_skip, 'ssm_norm_g': ssm_norm_g, 'w_out': w_out,
            'norm_cross_g': norm_cross_g, 'norm_mem_g': norm_mem_g,
            'w_cq': w_cq, 'w_ck': w_ck, 'w_cv': w_cv, 'w_co': w_co,
            'norm_ffn_g': norm_ffn_g, 'w_gate': w_gate, 'w_up': w_up, 'w_down': w_down,
            'final_norm_g': final_norm_g}


def reference(x_prompt, x_sample, mem_prompt, cache_k, cache_v, cache_logf, page_table,
              cache_mem_k, cache_mem_v, state_conv, state_ssm,
              norm_mix_g, w_in, b_forget, conv_w, conv_b, dt_bias, a_log, d_skip,
              ssm_norm_g, w_out, norm_cross_g, norm_mem_g, w_cq, w_ck, w_cv, w_co,
              norm_ffn_g, w_gate, w_up, w_down, final_norm_g):
    xp, xs = x_prompt, x_sample
    bp = xp.shape[0]
    kp_l, vp_l, fp_l, mkp_l, mvp_l, cp_l, sp_l = [], [], [], [], [], [], []
    ks_l, vs_l, fs_l, cs_l, ss_l = [], [], [], [], []
    for layer in range(DEPTH):
        mix_w = (norm_mix_g[layer], w_in[layer], b_forget[layer], conv_w[layer], conv_b[layer],
                 dt_bias[layer], a_log[layer], d_skip[layer], ssm_norm_g[layer], w_out[layer])
        conv0 = jnp.zeros((bp, CONV_WIDTH - 1, CONV_CH), xp.dtype)
        h0 = jnp.zeros((bp, H_SSM, P_SSM, N_STATE), jnp.float32)
        xp, kp, vp, fp, cp, sp = mixer_sublayer(xp, fox_prompt, conv0, h0, *mix_w)
        ck, cv, cf = cache_k[layer], cache_v[layer], cache_logf[layer]

        def attend_sample(q, k, v, logf, ck=ck, cv=cv, cf=cf):
            return fox_sample(q, k, v, logf, ck, cv, cf, page_table)

        xs, ks, vs, fs, cs, ss = mixer_sublayer(xs, attend_sample, state_conv[layer],
                                                state_ssm[layer], *mix_w)
        mkp, mvp = memory_kv(mem_prompt, norm_mem_g[layer], w_ck[layer], w_cv[layer])
        xp = cross_sublayer(xp, mkp, mvp, norm_cross_g[layer], w_cq[layer], w_co[layer])
        xs = cross_sublayer(xs, cache_mem_k[layer], cache_mem_v[layer], norm_cross_g[layer],
                            w_cq[layer], w_co[layer])
        xp = ffn_sublayer(xp, norm_ffn_g[layer], w_gate[layer], w_up[layer], w_down[layer])
        xs = ffn_sublayer(xs, norm_ffn_g[layer], w_gate[layer], w_up[layer], w_down[layer])
        kp_l.append(kp); vp_l.append(vp); fp_l.append(fp); mkp_l.append(mkp); mvp_l.append(mvp)
        cp_l.append(cp); sp_l.append(sp)
        ks_l.append(ks); vs_l.append(vs); fs_l.append(fs); cs_l.append(cs); ss_l.append(ss)
    y_prompt = rms_norm(xp, final_norm_g)
    y_sample = rms_norm(xs, final_norm_g)
    return (y_prompt, y_sample,
            jnp.stack(kp_l), jnp.stack(vp_l), jnp.stack(fp_l), jnp.stack(mkp_l), jnp.stack(mvp_l),
            jnp.stack(cp_l), jnp.stack(sp_l),
            jnp.stack(ks_l), jnp.stack(vs_l), jnp.stack(fs_l), jnp.stack(cs_l), jnp.stack(ss_l))
```

```python
import contextlib
import numpy as np
import concourse.bass as bass
import concourse.mybir as mybir
from concourse.bass_utils import run_bass_kernel_spmd

F32 = mybir.dt.float32
BF16 = mybir.dt.bfloat16
I32 = mybir.dt.int32
AF = mybir.ActivationFunctionType
ALU = mybir.AluOpType
AX = mybir.AxisListType

PIPE = 1
FULL_CFG = dict(NPRE=48, NOWN=16, NSEQ=16, NPG=16, NPHYS=2560)
EPS = 1e-6
ATT_SCALE = 64 ** -0.5
X_SCALE = 256 ** -0.5
DFF = 2816


class Builder:
    def __init__(self):
        self.nc = bass.Bass("TRN2", target_bir_lowering=False)
        self.es = contextlib.ExitStack()
        nc = self.nc
        self.eng = {"pe": nc.tensor, "act": nc.scalar, "dve": nc.vector, "pool": nc.gpsimd, "sp": nc.sync}
        self.semobj = {}
        self.cnt = {}
        for e in ("pe", "act", "dve", "pool"):
            self.semobj["s_" + e] = self.es.enter_context(nc.semaphore("s_" + e))
            self.cnt[e] = 0
        self.dcnt = {}
        self.tr = {}
        self.waited = {e: {} for e in self.eng}

    def sb(self, name, shape, dt, stack=None):
        self.uid = getattr(self, "uid", 0) + 1
        return (stack or self.es).enter_context(self.nc.sbuf_tensor("sb%d_%s" % (self.uid, name), shape, dt))

    def ps(self, name, shape, dt, stack=None):
        self.uid = getattr(self, "uid", 0) + 1
        return (stack or self.es).enter_context(self.nc.psum_tensor("ps%d_%s" % (self.uid, name), shape, dt))

    def dram(self, name, shape, dt, kind="Internal"):
        return self.nc.dram_tensor(name, shape, dt, kind=kind).ap()

    def _deps(self, R, W):
        d = {}

        def add(x):
            if x is not None:
                d[x[0]] = max(d.get(x[0], 0), x[1])
        for k in R:
            t = self.tr.get(k)
            if t:
                add(t[0])
        for k in W:
            t = self.tr.get(k)
            if t:
                add(t[0])
                for sn, v in t[1].items():
                    add((sn, v))
        return d

    def _wait(self, e, d, skip_self=False):
        for sn, v in d.items():
            if skip_self and sn == "s_" + e:
                continue
            if self.waited[e].get(sn, 0) >= v:
                continue
            self.eng[e].wait_ge(self.semobj[sn], v)
            self.waited[e][sn] = v

    def _mark(self, R, W, tag):
        for k in W:
            self.tr[k] = [tag, {}]
        for k in R:
            t = self.tr.setdefault(k, [None, {}])
            t[1][tag[0]] = max(t[1].get(tag[0], 0), tag[1])

    def op(self, e, fn, R=(), W=(), inc=True):
        d = self._deps(R, W)
        self._wait(e, d, skip_self=(e == "pe"))
        ins = fn(self.eng[e])
        if inc:
            self.cnt[e] += 1
            ins.then_inc(self.semobj["s_" + e], 1)
            tag = ("s_" + e, self.cnt[e])
        else:
            tag = ("s_" + e, self.cnt[e] + 1)
        self._mark(R, W, tag)

    def dma(self, q, out, in_, R=(), W=(), sem=None, **kw):
        d = self._deps(R, W)
        if sem not in self.semobj:
            self.semobj[sem] = self.es.enter_context(self.nc.semaphore(sem))
            self.dcnt[sem] = 0
        if self.dcnt[sem] > 0:
            d[sem] = max(d.get(sem, 0), self.dcnt[sem])
        self._wait(q, d)
        self.dcnt[sem] += 16
        self.eng[q].dma_start(out=out, in_=in_, **kw).then_inc(self.semobj[sem], 16)
        self._mark(R, W, (sem, self.dcnt[sem]))

    def gather(self, out, table, idx_ap, R=(), W=(), sem=None):
        d = self._deps(R, W)
        if sem not in self.semobj:
            self.semobj[sem] = self.es.enter_context(self.nc.semaphore(sem))
            self.dcnt[sem] = 0
        if self.dcnt[sem] > 0:
            d[sem] = max(d.get(sem, 0), self.dcnt[sem])
        self._wait("pool", d)
        self.dcnt[sem] += 16
        self.nc.gpsimd.indirect_dma_start(
            out=out, out_offset=None, in_=table,
            in_offset=bass.IndirectOffsetOnAxis(ap=idx_ap, axis=0),
        ).then_inc(self.semobj[sem], 16)
        self._mark(R, W, (sem, self.dcnt[sem]))

    def barrier(self):
        d = {"s_" + e: c for e, c in self.cnt.items() if c > 0}
        for s, c in self.dcnt.items():
            if c > 0:
                d[s] = c
        for e in self.eng:
            self._wait(e, dict(d))
        self.tr = {}


def build(cfg=FULL_CFG):
    b = Builder()
    nc = b.nc
    P = 128
    NPRE, NOWN, NSEQ, NPG, NPHYS = cfg["NPRE"], cfg["NOWN"], cfg["NSEQ"], cfg["NPG"], cfg["NPHYS"]

    def din(name, shape, dt=F32):
        return b.dram(name, shape, dt, kind="ExternalInput")

    def dout(name, shape, dt=F32):
        return b.dram(name, shape, dt, kind="ExternalOutput")

    xo = din("xo", [NOWN * P, 1024])
    xpre = din("xpre", [NPRE * P, 1024])
    validd = din("valid", [P, NPRE])
    xsd = din("xs", [NSEQ * 4, 1024])
    memd = din("mem", [256, 1024])
    ckvd = din("ckv", [NPHYS * 128, 1032])
    ptd = din("pt", [1, NSEQ * NPG], I32)
    cmkd = din("cmk", [NSEQ, 256, 1024])
    cmvd = din("cmv", [NSEQ, 256, 1024])
    sconvd = din("sconv", [NSEQ, 3, 1024])
    sssmd = din("sssm", [NSEQ, 512, 128])
    w_in = din("w_in", [1024, 3088])
    w_out = din("w_out", [1024, 1024])
    w_cq = din("w_cq", [1024, 1024])
    w_ck = din("w_ck", [1024, 1024])
    w_cv = din("w_cv", [1024, 1024])
    w_co = din("w_co", [1024, 1024])
    w_gate = din("w_gate", [1024, DFF])
    w_up = din("w_up", [1024, DFF])
    w_down = din("w_down", [DFF, 1024])
    gcold = din("gcol", [P, 40])
    rvsd = din("rvs", [1, 32])
    convbd = din("convb", [1, 1024])
    convwd = din("convw", [1, 4096])
    gfind = din("gfin", [1, 1024])
    cstd = din("cst", [P, 1024])

    y_o = dout("y_o", [NOWN * P, 1024])
    k_o = dout("k_o", [NOWN * P, 512])
    v_o = dout("v_o", [NOWN * P, 512])
    lf_o = dout("lf_o", [NOWN * P, 8])
    memk_o = dout("memk_o", [256, 1024])
    memv_o = dout("memv_o", [256, 1024])
    conv_o = dout("conv_o", [3, 1024])
    ssm_o = dout("ssm_o", [512, 128])
    y_s = dout("y_s", [NSEQ * 4, 1024])
    k_s = dout("k_s", [NSEQ * 4, 512])
    v_s = dout("v_s", [NSEQ * 4, 512])
    lf_s = dout("lf_s", [NSEQ * 4, 8])
    conv_s = dout("conv_s", [NSEQ, 3, 1024])
    ssm_s = dout("ssm_s", [NSEQ, 512, 128])

    NB_P = NPRE + NOWN
    kT_p = b.dram("kT_p", [512, NB_P * P], BF16)
    va_p = b.dram("va_p", [NB_P, P, 8 * 65], BF16)
    qT_p = b.dram("qT_p", [512, NOWN * P], BF16)
    qa_p = b.dram("qa_p", [24, NOWN * P], BF16)
    mixed_p = b.dram("mixed_p", [NOWN * P, 1024], BF16)
    NB_S = NPG + 1
    mixed_s = b.dram("mixed_s", [NSEQ, P, 1024], BF16)
    x2_p = b.dram("x2_p", [NOWN * P, 1024], F32)
    x2_s = b.dram("x2_s", [P, 1024], F32)

    cst = b.sb("cst", [P, 1024], F32)
    identf = cst[:, 0:128]
    ut = cst[:, 128:256]
    sl = cst[:, 256:384]
    iota = cst[:, 512:513]
    identb = b.sb("identb", [P, P], BF16)
    negmb = b.sb("negmb", [P, P], BF16)
    onesf = b.sb("onesf", [P, P], F32)
    epsc = b.sb("epsc", [P, 1], F32)
    gcol = b.sb("gcol", [P, 40], F32)
    rvs = b.sb("rvs", [P, 32], F32)
    nexpA = b.sb("nexpA", [P, 8], F32)
    convb = b.sb("convb", [P, 1024], F32)
    validt = b.sb("validt", [P, NPRE], F32)
    negc_p = b.sb("negc_p", [P, NB_P, 8], F32)
    idx = b.sb("idx", [P, NSEQ * NPG], I32)

    F = [b.ps("F%d" % i, [P, 512], F32) for i in range(6)]
    T = [b.ps("T%d" % i, [P, 1024], BF16) for i in range(2)]
    FK = ["F%d" % i for i in range(6)]
    TK = ["T0", "T1"]

    def ld(q, out, in_, W, sem, R=()):
        b.dma(q, out, in_, R=R, W=W, sem=sem)

    ld("sp", cst[:], cstd, ["cst"], "d_cst")
    ld("sp", gcol[:], gcold, ["gcol"], "d_gcol")
    ld("sp", rvs[:], rvsd.partition_broadcast(P), ["rvs"], "d_rvs")
    ld("sp", convb[:], convbd.partition_broadcast(P), ["convb"], "d_convb")
    ld("sp", validt[:], validd, ["validt"], "d_valid")
    b.op("dve", lambda e: e.tensor_copy(out=identb[:], in_=identf), R=["cst"], W=["identb"])
    b.op("dve", lambda e: e.tensor_copy(out=negmb[:], in_=cst[:, 384:512]), R=["cst"], W=["negmb"])
    b.op("dve", lambda e: e.memset(onesf[:], 1.0), W=["onesf"])
    b.op("dve", lambda e: e.memset(epsc[:], EPS), W=["epsc"])
    b.op("act", lambda e: e.activation(out=nexpA[:], in_=rvs[:, 16:24], func=AF.Exp), R=["rvs"], W=["nexpA"])
    b.op("dve", lambda e: e.tensor_scalar(out=nexpA[:], in0=nexpA[:], scalar1=-1.0, scalar2=None, op0=ALU.mult), R=["nexpA"], W=["nexpA"])
    with contextlib.ExitStack() as st0:
        pti = b.sb("pti", [P, NSEQ * NPG], I32, st0)
        ptf = b.sb("ptf", [P, NSEQ * NPG], F32, st0)
        ld("sp", pti[:], ptd.partition_broadcast(P), ["pti"], "d_pt")
        b.op("dve", lambda e: e.tensor_copy(out=ptf[:], in_=pti[:]), R=["pti"], W=["ptf"])
        b.op("dve", lambda e: e.tensor_scalar(out=ptf[:], in0=ptf[:], scalar1=128.0, scalar2=iota, op0=ALU.mult, op1=ALU.add), R=["ptf", "cst"], W=["ptf"])
        b.op("dve", lambda e: e.tensor_copy(out=idx[:], in_=ptf[:]), R=["ptf"], W=["idx"])
        b.barrier()

    stage_cnt = [0]

    def load_weight(stk_stage, dst, wd, din_chunks, c0, c1, gc=None, dst_off=0, key=None):
        bw = c1 - c0
        s = stage_cnt[0] % 2
        stage_cnt[0] += 1
        stg = stk_stage[s]
        sk = "stage%d" % s
        view = stg[:, 0:din_chunks * bw].rearrange("p (c n) -> p c n", c=din_chunks)
        ld("sp", view, wd.rearrange("(c p) n -> p c n", p=P)[:, :, c0:c1], [sk], "d_" + sk)
        for c in range(din_chunks):
            if c % 2 == 0:
                if gc is not None:
                    b.op("dve", lambda e, c=c: e.tensor_scalar(out=dst[:, c, dst_off:dst_off + bw], in0=view[:, c, :], scalar1=gc[:, c:c + 1], scalar2=None, op0=ALU.mult), R=[sk, "gcol"], W=[key])
                else:
                    b.op("dve", lambda e, c=c: e.tensor_copy(out=dst[:, c, dst_off:dst_off + bw], in_=view[:, c, :]), R=[sk], W=[key])
            else:
                if gc is not None:
                    b.op("act", lambda e, c=c: e.activation(out=dst[:, c, dst_off:dst_off + bw], in_=view[:, c, :], func=AF.Copy, scale=gc[:, c:c + 1]), R=[sk, "gcol"], W=[key])
                else:
                    b.op("act", lambda e, c=c: e.copy(out=dst[:, c, dst_off:dst_off + bw], in_=view[:, c, :]), R=[sk], W=[key])
        return view, sk

    def rms_rstd(src, width, tmp, ms, key_src, key_tmp, key_ms):
        b.op("act", lambda e: e.activation(out=tmp, in_=src, func=AF.Square, scale=float(width) ** -0.5, accum_out=ms[:, 0:1]), R=[key_src], W=[key_tmp, key_ms])
        b.op("act", lambda e: e.activation(out=ms[:, 0:1], in_=ms[:, 0:1], func=AF.Sqrt, bias=epsc[:, 0:1]), R=[key_ms], W=[key_ms])
        b.op("dve", lambda e: e.reciprocal(out=ms[:, 0:1], in_=ms[:, 0:1]), R=[key_ms], W=[key_ms])

    def transposes(dstT, tk, src, nch, src_key, ident=None):
        for c in range(nch):
            b.op("pe", lambda e, c=c: e.transpose(out=dstT[:, c, :], in_=src[:, c * P:(c + 1) * P], identity=identb[:]),
                 R=[src_key, "identb"], W=[tk], inc=(c == nch - 1))

    def mm_group(out_ap, okey, pairs, R):
        n = len(pairs)
        for i, (l, r) in enumerate(pairs):
            b.op("pe", lambda e, l=l, r=r, i=i: e.matmul(out_ap, lhsT=l, rhs=r, start=(i == 0), stop=(i == n - 1)),
                 R=R, W=[okey], inc=(i == n - 1))

    with contextlib.ExitStack() as s1:
        winb = b.sb("winb", [P, 8, 3088], BF16, s1)
        wcv = b.sb("wcv", [P, 4, 8, 1024], BF16, s1)
        with contextlib.ExitStack() as s1a:
            stage = [b.sb("stage%d" % i, [P, 8 * 512], F32, s1a) for i in range(2)]
            for c0 in range(0, 3088, 512):
                c1 = min(c0 + 512, 3088)
                load_weight(stage, winb, w_in, 8, c0, c1, gc=gcol[:, 0:8], dst_off=c0, key="winb")
            convwb = b.sb("convwb", [P, 4096], F32, s1a)
            ld("sp", convwb[:], convwd.partition_broadcast(P), ["convwb"], "d_convwb")
            for blk in range(2):
                s = stage_cnt[0] % 2
                stage_cnt[0] += 1
                sk = "stage%d" % s
                view = stage[s][:].rearrange("p (c n) -> p c n", c=8)
                ld("sp", view, w_in.rearrange("(c p) n -> p c n", p=P)[:, :, 2056 + blk * 512:2056 + (blk + 1) * 512], [sk], "d_" + sk)
                for j in range(4):
                    for c in range(8):
                        eng = "dve"
                        b.op(eng, lambda e, j=j, c=c, blk=blk, view=view: e.scalar_tensor_tensor(
                            out=wcv[:, j, c, blk * 512:(blk + 1) * 512], in0=view[:, c, :], scalar=gcol[:, c:c + 1],
                            in1=convwb[:, j * 1024 + blk * 512:j * 1024 + (blk + 1) * 512], op0=ALU.mult, op1=ALU.mult),
                            R=[sk, "gcol", "convwb"], W=["wcv"])
            b.barrier()

        xt = [b.sb("xt%d" % i, [P, 1024], F32, s1) for i in range(2)]
        ms = b.sb("ms", [P, 4], F32, s1)
        xn = b.sb("xn", [P, 1024], BF16, s1)
        hT = [b.sb("hT%d" % i, [P, 8, 131], BF16, s1) for i in range(2)]
        qb = b.sb("qb", [P, 512], BF16, s1)
        kf = b.sb("kf", [P, 512], F32, s1)
        kb = b.sb("kb", [P, 512], BF16, s1)
        vf = b.sb("vf", [P, 512], F32, s1)
        vaug = b.sb("vaug", [P, 8, 65], BF16, s1)
        mdiag = b.sb("mdiag", [P, 8], F32, s1)
        trs = b.sb("trs", [P, 4, P], BF16, s1)
        t16 = b.sb("t16", [P, 16], F32, s1)
        e16 = b.sb("e16", [P, 16], F32, s1)
        lf = b.sb("lf", [P, 8], F32, s1)
        dtv = b.sb("dtv", [P, 8], F32, s1)
        ncarry = b.sb("ncarry", [P, 8], F32, s1)
        cm = b.sb("cm", [P, 8], F32, s1)
        r1 = b.sb("r1", [P, 8], F32, s1)
        aug = b.sb("aug", [P, 128], BF16, s1)
        augT = b.sb("augT", [P, P], BF16, s1)
        sz = b.sb("sz", [P, 512], F32, s1)
        pre = b.sb("pre", [P, 512], F32, s1)
        xact = b.sb("xact", [P, 512], BF16, s1)
        bcact = b.sb("bcact", [P, 512], BF16, s1)
        av = b.sb("av", [P, 8], F32, s1)
        E24 = b.sb("E24", [P, 24], F32, s1)
        xdt = b.sb("xdt", [P, 8, 64], BF16, s1)
        xdd = b.sb("xdd", [P, 8, 64], BF16, s1)
        bcT = b.sb("bcT", [P, 4, P], BF16, s1)
        S = b.sb("S", [P, 512], F32, s1)
        Sb = b.sb("Sb", [P, 512], BF16, s1)
        Rm = b.sb("Rm", [P, 8, P], F32, s1)
        Rflat = Rm[:].rearrange("p h t -> p (h t)")
        qk = Rflat[:, 0:512]
        ytmp = Rflat[:, 512:1024]
        rawx = Rflat
        Dm = b.sb("Dm", [P, 8, P], BF16, s1)
        GTm = b.sb("GTm", [P, 2, P], F32, s1)
        Mm = b.sb("Mm", [P, 8, P], BF16, s1)
        ysb = b.sb("ysb", [P, 512], F32, s1)
        yn = b.sb("yn", [P, 512], BF16, s1)
        scw = b.sb("scw", [P, 1024], BF16, s1)
        scf = b.sb("scf", [P, 1024], F32, s1)
        cw9 = b.sb("cw9", [P, 1024], F32, s1)
        sh9 = b.sb("sh9", [P, P], BF16, s1)
        sst = b.sb("sst", [P, 4, P], F32, s1)
        pg = [b.sb("pg%d" % i, [P, 1032], F32, s1) for i in range(2)]
        kTp = [b.sb("kTp%d" % i, [P, 4, P], BF16, s1) for i in range(2)]
        vbp = [b.sb("vbp%d" % i, [P, 8, 65], BF16, s1) for i in range(2)]
        rkp = [b.sb("rkp%d" % i, [P, 8], F32, s1) for i in range(2)]
        Pp = [b.sb("Pp%d" % i, [P, 32], BF16, s1) for i in range(2)]
        sb32 = b.sb("sb32", [P, 32], F32, s1)
        rkn = b.sb("rkn", [P, 8], F32, s1)
        pcarry = b.sb("pcarry", [P, 8], F32, s1)
        Qbd = b.sb("Qbd", [P, 4, 8], BF16, s1)
        A24 = b.sb("A24", [P, 32], BF16, s1)
        negm32 = b.sb("negm32", [P, 32], BF16, s1)
        onesb = b.sb("onesb", [P, P], BF16, s1)
        rl8 = b.sb("rl8", [P, 8], F32, s1)
        ot = b.sb("ot", [P, 8, 64], BF16, s1)
        for t_ in (xn, aug, vaug, trs, Rm, scw, scf, cw9, sh9, yn, Qbd, A24, kTp[0], kTp[1], ot):
            b.op("pool", lambda e, t_=t_: e.memset(t_[:], 0.0), W=["init"])
        for t_ in (vbp[0], vbp[1], onesb):
            b.op("pool", lambda e, t_=t_: e.memset(t_[:], 1.0), W=["init"])
        b.op("dve", lambda e: e.tensor_copy(out=negm32[:], in_=cst[:, 528:560]), R=["cst"], W=["init"])
        b.barrier()
        for k in range(3):
            ld("sp", cw9[3 * k:3 * k + 3, :], convwd[0:1, 0:3072].rearrange("o (j n) -> (o j) n", j=3), ["cw9"], "d_cw9")
        b.op("dve", lambda e: e.tensor_copy(out=sh9[0:9, :], in_=cst[0:9, 897:1025 - 0] if False else cst[0:9, 896:1024]), R=["cst"], W=["sh9"])
        b.barrier()

        tile_no = [0]

        def kv_post(ctx, blk, ksrc, ksrc_key, vsrc, vsrc_key, lf_ap, lf_key, vcol):
            b.op("act", lambda e: e.copy(out=kb[:], in_=ksrc), R=[ksrc_key], W=["kb"])
            tv = T[1][:, 0:512].rearrange("p (c t) -> p c t", c=4)
            transposes(tv, "T1", kb, 4, "kb")
            b.op("dve", lambda e: e.tensor_copy(out=trs[:], in_=tv), R=["T1"], W=["trs"])
            if not ctx.get("nostore"):
                ld("pool", ctx["kT"].rearrange("(c p) t -> p c t", p=P)[:, :, blk * P:(blk + 1) * P], trs[:], [ctx["name"] + "kT"], "d_trs", R=["trs"])
            b.op("dve", lambda e: e.tensor_copy(out=vaug[:, :, 0:64], in_=vsrc.rearrange("p (h d) -> p h d", h=8)), R=[vsrc_key], W=["vaug"])
            if vcol is None:
                b.op("dve", lambda e: e.memset(vaug[:, :, 64:65], 1.0), W=["vaug"])
            else:
                b.op("dve", lambda e: e.tensor_copy(out=vaug[:, :, 64:65], in_=vcol.unsqueeze(2).broadcast_to([P, 8, 1])), R=["validt"], W=["vaug"])
            if not ctx.get("nostore"):
                ld("pool", ctx["va"][blk].rearrange("p (h d) -> p h d", h=8), vaug[:], [ctx["name"] + "va"], "d_vaug", R=["vaug"])
            b.op("pe", lambda e: e.matmul(F[4][:, 16:24], lhsT=ut, rhs=lf_ap, start=True, stop=True), R=["cst", lf_key], W=["F4"], inc=False)
            b.op("pe", lambda e: e.matmul(F[4][:, 24:32], lhsT=onesf[:], rhs=lf_ap, start=True, stop=True), R=["onesf", lf_key], W=["F4"])
            nc_blk = ctx["negc"](blk)
            b.op("dve", lambda e: e.scalar_tensor_tensor(out=nc_blk, in0=F[4][:, 16:24], scalar=-1.0, in1=ncarry[:], op0=ALU.mult, op1=ALU.add), R=["F4", "ncarry"], W=[ctx["negk"]])
            b.op("dve", lambda e: e.scalar_tensor_tensor(out=ncarry[:], in0=F[4][:, 24:32], scalar=-1.0, in1=ncarry[:], op0=ALU.mult, op1=ALU.add), R=["F4", "ncarry"], W=["ncarry"])

        def mix_tile(ctx, kind, x_src, blk, qi, vcol, first, out_rows):
            full = kind != "P"
            s = tile_no[0] % 2
            ps_ = 1 - s
            tile_no[0] += 1
            xk, hk = "xt%d" % s, "hT%d" % s
            if kind == "S":
                ld("sp", xt[s][0:4, :], x_src, [xk], "d_" + xk)
            else:
                ld("sp", xt[s][:], x_src, [xk], "d_" + xk)
            rms_rstd(xt[s][:], 1024, xn[:], ms, xk, "xn", "ms")
            b.op("act", lambda e: e.activation(out=xn[:], in_=xt[s][:], func=AF.Copy, scale=ms[:, 0:1]), R=[xk, "ms"], W=["xn"])
            tv0 = T[0][:].rearrange("p (c t) -> p c t", c=8)
            transposes(tv0, "T0", xn, 8, "xn")
            b.op("dve", lambda e: e.tensor_copy(out=hT[s][:, :, 3:131], in_=tv0), R=["T0"], W=[hk])
            if first:
                b.op("dve", lambda e: e.memset(hT[s][:, :, 0:3], 0.0), W=[hk])
            else:
                b.op("dve", lambda e: e.tensor_copy(out=hT[s][:, :, 0:3], in_=hT[ps_][:, :, 128:131]), R=["hT%d" % ps_], W=[hk])
            fb = [0]

            def tm(c0, c1):
                i = fb[0] % 2
                fb[0] += 1
                mm_group(F[i][:, 0:c1 - c0], FK[i], [(hT[s][:, c, 3:131], winb[:, c, c0:c1]) for c in range(8)], [hk, "winb"])
                return F[i], FK[i]

            def tmconv(blk2):
                i = fb[0] % 2
                fb[0] += 1
                pairs = [(hT[s][:, c, j:j + 128], wcv[:, j, c, blk2 * 512:(blk2 + 1) * 512]) for j in range(4) for c in range(8)]
                Rk = [hk, "wcv"]
                if kind == "S":
                    pairs += [(sh9[0:9, :], scw[0:9, blk2 * 512:(blk2 + 1) * 512])]
                    Rk += ["sh9", "scw"]
                mm_group(F[i][:, :], FK[i], pairs, Rk)
                return F[i], FK[i]

            if full:
                pq, pqk = tm(0, 512)
                b.op("act", lambda e: e.activation(out=qb[:], in_=pq[:, :], func=AF.Copy, scale=ATT_SCALE), R=[pqk], W=["qb"])
            pk, pkk = tm(512, 1024)
            b.op("dve", lambda e: e.tensor_copy(out=kf[:], in_=pk[:, :]), R=[pkk], W=["kf"])
            if full:
                ld("pool", out_rows["k"], kf[0:out_rows["n"], :], [], "d_kf", R=["kf"])
            pv, pvk = tm(1024, 1536)
            if full:
                b.op("act", lambda e: e.copy(out=vf[:], in_=pv[:, :]), R=[pvk], W=["vf"])
                ld("pool", out_rows["v"], vf[0:out_rows["n"], :], [], "d_vf", R=["vf"])
            mm_group(F[4][:, 0:8], "F4", [(hT[s][:, c, 3:131], winb[:, c, 1536:1544]) for c in range(8)], [hk, "winb"])
            mm_group(F[4][:, 8:16], "F4", [(hT[s][:, c, 3:131], winb[:, c, 3080:3088]) for c in range(8)], [hk, "winb"])
            b.op("dve", lambda e: e.tensor_tensor(out=t16[:], in0=F[4][:, 0:16], in1=rvs[:, 0:16], op=ALU.add), R=["F4", "rvs"], W=["t16"])
            b.op("act", lambda e: e.activation(out=e16[:, 0:8], in_=t16[:, 0:8], func=AF.Exp, scale=-1.0), R=["t16"], W=["e16"])
            b.op("act", lambda e: e.activation(out=e16[:, 8:16], in_=t16[:, 8:16], func=AF.Exp), R=["t16"], W=["e16"])
            b.op("act", lambda e: e.activation(out=e16[:], in_=e16[:], func=AF.Ln, bias=1.0), R=["e16"], W=["e16"])
            if vcol is None:
                b.op("dve", lambda e: e.tensor_scalar(out=lf[:], in0=e16[:, 0:8], scalar1=-1.0, scalar2=None, op0=ALU.mult), R=["e16"], W=["lf"])
                b.op("dve", lambda e: e.tensor_copy(out=dtv[:], in_=e16[:, 8:16]), R=["e16"], W=["dtv"])
            else:
                b.op("dve", lambda e: e.tensor_scalar(out=lf[:], in0=e16[:, 0:8], scalar1=-1.0, scalar2=vcol, op0=ALU.mult, op1=ALU.mult), R=["e16", "validt"], W=["lf"])
                b.op("dve", lambda e: e.tensor_scalar(out=dtv[:], in0=e16[:, 8:16], scalar1=vcol, scalar2=None, op0=ALU.mult), R=["e16", "validt"], W=["dtv"])
            if full:
                ld("pool", out_rows["lf"], lf[0:out_rows["n"], :], [], "d_lf", R=["lf"])
            kv_post(ctx, blk, kf[:], "kf", pv[:, :], pvk, lf[:], "lf", vcol)
            if full:
                b.op("dve", lambda e: e.tensor_tensor(out=qk, in0=qb[:], in1=kb[:], op=ALU.mult), R=["qb", "kb"], W=["Rm"])
                b.op("dve", lambda e: e.tensor_reduce(out=mdiag[:], in_=qk.rearrange("p (h d) -> p h d", h=8), axis=AX.X, op=ALU.add), R=["Rm"], W=["mdiag"])
                nc_blk = ctx["negc"](blk)
                b.op("dve", lambda e: e.scalar_tensor_tensor(out=cm[:], in0=nc_blk, scalar=-1.0, in1=mdiag[:], op0=ALU.mult, op1=ALU.subtract), R=[ctx["negk"], "mdiag"], W=["cm"])
                b.op("dve", lambda e: e.tensor_copy(out=aug[:, 0:8], in_=cm[:]), R=["cm"], W=["aug"])
                b.op("dve", lambda e: e.tensor_tensor(out=r1[:], in0=cm[:], in1=aug[:, 0:8], op=ALU.subtract), R=["cm", "aug"], W=["r1"])
                b.op("dve", lambda e: e.tensor_copy(out=aug[:, 8:16], in_=r1[:]), R=["r1"], W=["aug"])
                b.op("dve", lambda e: e.tensor_tensor(out=r1[:], in0=r1[:], in1=aug[:, 8:16], op=ALU.subtract), R=["r1", "aug"], W=["r1"])
                b.op("dve", lambda e: e.tensor_copy(out=aug[:, 16:24], in_=r1[:]), R=["r1"], W=["aug"])
                b.op("pe", lambda e: e.transpose(out=T[1][:, 512:640], in_=aug[:, :], identity=identb[:]), R=["aug", "identb"], W=["T1"])
                b.op("dve", lambda e: e.tensor_copy(out=augT[:], in_=T[1][:, 512:640]), R=["T1"], W=["augT"])
                tv = T[1][:, 0:512].rearrange("p (c t) -> p c t", c=4)
                if kind == "S":
                    b.op("dve", lambda e: e.tensor_tensor(out=A24[0:24, :].rearrange("p (h q) -> p h q", h=8), in0=augT[0:24, 0:4].unsqueeze(1).broadcast_to([24, 8, 4]),
                                                          in1=cst[0:24, 520:528].unsqueeze(2).broadcast_to([24, 8, 4]), op=ALU.mult), R=["augT", "cst"], W=["A24"])
                    transposes(tv, "T1", qb, 4, "qb")
                    b.op("dve", lambda e: e.tensor_copy(out=Qbd[0:64, :, 0:4], in_=tv[0:64, :, 0:4]), R=["T1"], W=["Qbd"])
                    b.op("dve", lambda e: e.tensor_copy(out=Qbd[64:128, :, 4:8], in_=tv[64:128, :, 0:4]), R=["T1"], W=["Qbd"])
                else:
                    ld("pool", ctx["qa"][:, qi * P:(qi + 1) * P], augT[0:24, :], [ctx["name"] + "qa"], "d_augT", R=["augT"])
                    transposes(tv, "T1", qb, 4, "qb")
                    b.op("dve", lambda e: e.tensor_copy(out=trs[:], in_=tv), R=["T1"], W=["trs"])
                    ld("pool", ctx["qT"].rearrange("(c p) t -> p c t", p=P)[:, :, qi * P:(qi + 1) * P], trs[:], [ctx["name"] + "qT"], "d_trsq", R=["trs"])
                pz, pzk = tm(1544, 2056)
                b.op("act", lambda e: e.activation(out=sz[:], in_=pz[:, :], func=AF.Silu), R=[pzk], W=["sz"])
                if out_rows.get("conv") is not None:
                    for hb in range(2):
                        pr, prk = tm(2056 + hb * 512, 2056 + (hb + 1) * 512)
                        b.op("act", lambda e, hb=hb: e.copy(out=rawx[:, hb * 512:(hb + 1) * 512], in_=pr[:, :]), R=[prk], W=["Rm"])
                    r0, r1_ = out_rows["convrows"]
                    ld("pool", out_rows["conv"], rawx[r0:r1_, :], [], "d_rawx", R=["Rm"])
            px, pxk = tmconv(0)
            b.op("dve", lambda e: e.tensor_tensor(out=pre[:], in0=px[:, :], in1=convb[:, 0:512], op=ALU.add), R=[pxk, "convb"], W=["pre"])
            b.op("act", lambda e: e.activation(out=xact[:], in_=pre[:], func=AF.Silu), R=["pre"], W=["xact"])
            pbc, pbck = tmconv(1)
            b.op("dve", lambda e: e.tensor_tensor(out=pre[:], in0=pbc[:, :], in1=convb[:, 512:1024], op=ALU.add), R=[pbck, "convb"], W=["pre"])
            b.op("act", lambda e: e.activation(out=bcact[:], in_=pre[:], func=AF.Silu), R=["pre"], W=["bcact"])
            b.op("dve", lambda e: e.tensor_tensor(out=av[:], in0=dtv[:], in1=nexpA[:], op=ALU.mult), R=["dtv", "nexpA"], W=["av"])
            b.op("pe", lambda e: e.matmul(F[4][:, 32:40], lhsT=sl, rhs=av[:], start=True, stop=True), R=["cst", "av"], W=["F4"], inc=False)
            b.op("pe", lambda e: e.matmul(F[4][:, 40:48], lhsT=ut, rhs=av[:], start=True, stop=True), R=["cst", "av"], W=["F4"], inc=False)
            b.op("pe", lambda e: e.matmul(F[4][:, 48:56], lhsT=onesf[:], rhs=av[:], start=True, stop=True), R=["onesf", "av"], W=["F4"])
            b.op("act", lambda e: e.activation(out=E24[:], in_=F[4][:, 32:56], func=AF.Exp), R=["F4"], W=["E24"])
            xav = xact[:].rearrange("p (h d) -> p h d", h=8)
            b.op("dve", lambda e: e.tensor_tensor(out=xdt[:], in0=xav, in1=dtv[:].unsqueeze(2).broadcast_to([P, 8, 64]), op=ALU.mult), R=["xact", "dtv"], W=["xdt"])
            b.op("dve", lambda e: e.tensor_tensor(out=xdd[:], in0=xdt[:], in1=E24[:, 0:8].unsqueeze(2).broadcast_to([P, 8, 64]), op=ALU.mult), R=["xdt", "E24"], W=["xdd"])
            if full:
                tvb = T[1][:, 0:512].rearrange("p (c t) -> p c t", c=4)
                transposes(tvb, "T1", bcact, 4, "bcact")
                b.op("act", lambda e: e.copy(out=bcT[:], in_=tvb), R=["T1"], W=["bcT"])
                b.op("dve", lambda e: e.tensor_copy(out=Sb[:], in_=S[:]), R=["S"], W=["Sb"])
                for g in range(2):
                    b.op("pe", lambda e, g=g: e.matmul(F[1][:, g * 256:(g + 1) * 256], lhsT=bcT[:, 2 + g, :], rhs=Sb[:, g * 256:(g + 1) * 256], start=True, stop=True), R=["bcT", "Sb"], W=["F1"], inc=(g == 1))
                for g in range(2):
                    b.op("pe", lambda e, g=g: e.matmul(F[4][:, 256 + g * 128:256 + (g + 1) * 128], lhsT=bcT[:, g, :], rhs=bcT[:, 2 + g, :], start=True, stop=True), R=["bcT"], W=["F4"], inc=(g == 1))
                b.op("dve", lambda e: e.tensor_tensor(out=Rm[:], in0=ut.unsqueeze(1).broadcast_to([P, 8, P]), in1=av[:].unsqueeze(2).broadcast_to([P, 8, P]), op=ALU.mult), R=["cst", "av"], W=["Rm"])
                for hh in range(2):
                    b.op("pe", lambda e, hh=hh: e.matmul(F[2 + hh][:, :], lhsT=sl, rhs=Rm[:, hh * 4:(hh + 1) * 4, :].rearrange("p h t -> p (h t)"), start=True, stop=True), R=["cst", "Rm"], W=[FK[2 + hh]])
                    b.op("act", lambda e, hh=hh: e.activation(out=Dm[:, hh * 4:(hh + 1) * 4, :].rearrange("p h t -> p (h t)"), in_=F[2 + hh][:, :], func=AF.Exp), R=[FK[2 + hh]], W=["Dm"])
                b.op("dve", lambda e: e.tensor_tensor(out=GTm[:], in0=F[4][:, 256:512].rearrange("p (g t) -> p g t", g=2), in1=ut.unsqueeze(1).broadcast_to([P, 2, P]), op=ALU.mult), R=["F4", "cst"], W=["GTm"])
                for g in range(2):
                    b.op("dve", lambda e, g=g: e.tensor_tensor(out=Mm[:, 4 * g:4 * g + 4, :], in0=Dm[:, 4 * g:4 * g + 4, :], in1=GTm[:, g, :].unsqueeze(1).broadcast_to([P, 4, P]), op=ALU.mult), R=["Dm", "GTm"], W=["Mm"])
                for h in range(8):
                    b.op("pe", lambda e, h=h: e.matmul(F[0][:, h * 64:(h + 1) * 64], lhsT=Mm[:, h, :], rhs=xdt[:, h, :], start=True, stop=True), R=["Mm", "xdt"], W=["F0"], inc=(h == 7))
                y3 = ysb[:].rearrange("p (h d) -> p h d", h=8)
                b.op("dve", lambda e: e.tensor_tensor(out=y3, in0=F[1][:, :].rearrange("p (h d) -> p h d", h=8), in1=E24[:, 8:16].unsqueeze(2).broadcast_to([P, 8, 64]), op=ALU.mult), R=["F1", "E24"], W=["ysb"])
                b.op("dve", lambda e: e.tensor_tensor(out=ysb[:], in0=ysb[:], in1=F[0][:, :], op=ALU.add), R=["ysb", "F0"], W=["ysb"])
                b.op("dve", lambda e: e.tensor_tensor(out=ytmp.rearrange("p (h d) -> p h d", h=8), in0=xav, in1=rvs[:, 24:32].unsqueeze(2).broadcast_to([P, 8, 64]), op=ALU.mult), R=["xact", "rvs"], W=["Rm"])
                b.op("dve", lambda e: e.tensor_tensor(out=ysb[:], in0=ysb[:], in1=ytmp, op=ALU.add), R=["ysb", "Rm"], W=["ysb"])
                b.op("dve", lambda e: e.tensor_tensor(out=ysb[:], in0=ysb[:], in1=sz[:], op=ALU.mult), R=["ysb", "sz"], W=["ysb"])
                rms_rstd(ysb[:], 512, yn[:], ms[:, 1:2], "ysb", "yn", "ms")
                b.op("act", lambda e: e.activation(out=yn[:], in_=ysb[:], func=AF.Copy, scale=ms[:, 1:2]), R=["ysb", "ms"], W=["yn"])
                ld("pool", out_rows["mixed"], yn[:], [ctx["name"] + "mixed_ssm"], "d_yn", R=["yn"])
            for g in range(2):
                b.op("pe", lambda e, g=g: e.matmul(F[5][:, g * 256:(g + 1) * 256], lhsT=bcact[:, g * 128:(g + 1) * 128], rhs=xdd[:, 4 * g:4 * g + 4, :].rearrange("p h d -> p (h d)"), start=True, stop=True), R=["bcact", "xdd"], W=["F5"], inc=(g == 1))
            S3 = S[:].rearrange("p (h d) -> p h d", h=8)
            b.op("dve", lambda e: e.tensor_tensor(out=S3, in0=S3, in1=E24[:, 16:24].unsqueeze(2).broadcast_to([P, 8, 64]), op=ALU.mult), R=["S", "E24"], W=["S"])
            b.op("dve", lambda e: e.tensor_tensor(out=S[:], in0=S[:], in1=F[5][:, :], op=ALU.add), R=["S", "F5"], W=["S"])

        def state_out(dst):
            for c in range(4):
                b.op("pe", lambda e, c=c: e.transpose(out=F[5][:, c * P:(c + 1) * P], in_=S[:, c * P:(c + 1) * P], identity=identf), R=["S", "cst"], W=["F5"], inc=(c == 3))
            b.op("dve", lambda e: e.tensor_copy(out=sst[:].rearrange("p c n -> p (c n)"), in_=F[5][:, :]), R=["F5"], W=["sst"])
            ld("pool", dst.rearrange("(c p) n -> p c n", p=P), sst[:], [], "d_sst", R=["sst"])

        ctxp = {"name": "p", "kT": kT_p, "va": va_p, "qT": qT_p, "qa": qa_p, "negc": (lambda blk: negc_p[:, blk, :]), "negk": "negc_p"}
        b.op("dve", lambda e: e.memset(S[:], 0.0), W=["S"])
        b.op("dve", lambda e: e.memset(ncarry[:], 0.0), W=["ncarry"])
        for t in range(NPRE):
            mix_tile(ctxp, "P", xpre[t * P:(t + 1) * P, :], t, None, validt[:, t:t + 1], t == 0, None)
        for t in range(NOWN):
            orow = {"n": P, "k": k_o[t * P:(t + 1) * P, :], "v": v_o[t * P:(t + 1) * P, :], "lf": lf_o[t * P:(t + 1) * P, :],
                    "mixed": mixed_p[t * P:(t + 1) * P, 512:1024]}
            if t == NOWN - 1:
                orow["conv"] = conv_o
                orow["convrows"] = (125, 128)
            mix_tile(ctxp, "O", xo[t * P:(t + 1) * P, :], NPRE + t, t, None, False, orow)
        state_out(ssm_o)
        b.barrier()
        for t_ in (xt[0], xt[1]):
            b.op("dve", lambda e, t_=t_: e.memset(t_[:], 0.0), W=["xt0", "xt1"])
        blkc = [0]

        def s_block(kT_ap, kT_key, vb_ap, vb_key, rk_ap, rk_key, mask, first, last):
            n = blkc[0] % 2
            blkc[0] += 1
            sc, sck = F[n][:, 0:32], FK[n]
            b.op("pe", lambda e: e.matmul(sc, lhsT=onesb[0:24, :], rhs=A24[0:24, :], start=True, stop=False), R=["init", "A24"], W=[sck], inc=False)
            for c in range(4):
                b.op("pe", lambda e, c=c: e.matmul(sc[:, 8 * c:8 * c + 8], lhsT=kT_ap[:, c, :], rhs=Qbd[:, c, :], start=False, stop=(c == 3 and not mask)),
                     R=[kT_key, "Qbd"], W=[sck], inc=(c == 3 and not mask))
            if mask:
                b.op("pe", lambda e: e.matmul(sc, lhsT=identb[:], rhs=negm32[:], start=False, stop=True), R=["identb", "init"], W=[sck])
            b.op("dve", lambda e: e.tensor_tensor(out=sb32[:].rearrange("p (h q) -> p h q", h=8), in0=sc.rearrange("p (h q) -> p h q", h=8),
                                                  in1=rk_ap.unsqueeze(2).broadcast_to([P, 8, 4]), op=ALU.add), R=[sck, rk_key], W=["sb32"])
            b.op("act", lambda e: e.activation(out=Pp[n][:], in_=sb32[:], func=AF.Exp), R=["sb32"], W=["Pp%d" % n])
            for hh in range(2):
                b.op("pe", lambda e, hh=hh: e.matmul(F[2 + hh][0:32, 0:260], lhsT=Pp[n][:, 0:32], rhs=vb_ap[:, 4 * hh:4 * hh + 4, :].rearrange("p h d -> p (h d)"),
                                                    start=first, stop=last), R=["Pp%d" % n, vb_key], W=[FK[2 + hh]])

        order = [(i, j) for i in range(NSEQ) for j in reversed(range(NPG))]
        gi = [0]

        def ensure_gather(n):
            while gi[0] <= n and gi[0] < len(order):
                m_ = gi[0]
                g_ = order[m_][0] * NPG + order[m_][1]
                b.gather(pg[m_ % 2][:], ckvd, idx[:, g_:g_ + 1], R=["idx"], W=["pg%d" % (m_ % 2)], sem="g_pg%d" % (m_ % 2))
                gi[0] += 1

        def prep(n):
            sl_ = n % 2
            pk_ = "pg%d" % sl_
            b.op("act", lambda e: e.copy(out=kb[:], in_=pg[sl_][:, 0:512]), R=[pk_], W=["kb"])
            tv = T[1][:, 0:512].rearrange("p (c t) -> p c t", c=4)
            transposes(tv, "T1", kb, 4, "kb")
            b.op("dve", lambda e: e.tensor_copy(out=kTp[sl_][:], in_=tv), R=["T1"], W=["kTp%d" % sl_])
            b.op("pool", lambda e: e.tensor_copy(out=vbp[sl_][:, :, 0:64], in_=pg[sl_][:, 512:1024].rearrange("p (h d) -> p h d", h=8)), R=[pk_], W=["vbp%d" % sl_])
            b.op("pe", lambda e: e.matmul(F[4][:, 0:8], lhsT=sl, rhs=pg[sl_][:, 1024:1032], start=True, stop=True), R=["cst", pk_], W=["F4"], inc=False)
            b.op("pe", lambda e: e.matmul(F[4][:, 8:16], lhsT=onesf[:], rhs=pg[sl_][:, 1024:1032], start=True, stop=True), R=["onesf", pk_], W=["F4"])
            b.op("dve", lambda e: e.tensor_tensor(out=rkp[sl_][:], in0=F[4][:, 0:8], in1=pcarry[:], op=ALU.add), R=["F4", "pcarry"], W=["rkp%d" % sl_])
            b.op("dve", lambda e: e.tensor_tensor(out=pcarry[:], in0=F[4][:, 8:16], in1=pcarry[:], op=ALU.add), R=["F4", "pcarry"], W=["pcarry"])

        ensure_gather(0)
        for i in range(NSEQ):
            base = i * NPG
            ctx = {"name": "s%d" % i, "nostore": True, "negc": (lambda blk: rkn[:]), "negk": "rkn"}
            b.op("dve", lambda e: e.memset(ncarry[:], 0.0), W=["ncarry"])
            b.op("dve", lambda e: e.memset(pcarry[:], 0.0), W=["pcarry"])
            if PIPE:
                ensure_gather(base + 1)
                prep(base)
            ld("sp", sst[:], sssmd[i].rearrange("(c p) n -> p c n", p=P), ["sst"], "d_sstl")
            for c in range(4):
                b.op("pe", lambda e, c=c: e.transpose(out=F[5][:, c * P:(c + 1) * P], in_=sst[:, c, :], identity=identf), R=["sst", "cst"], W=["F5"], inc=(c == 3))
            b.op("dve", lambda e: e.tensor_copy(out=S[:], in_=F[5][:, :]), R=["F5"], W=["S"])
            for k in range(3):
                ld("sp", scf[3 * k:3 * k + 3, :], sconvd[i, k:k + 1, :].partition_broadcast(3), ["scf"], "d_scf%d" % k)
            b.op("dve", lambda e: e.tensor_tensor(out=scw[0:9, :], in0=scf[0:9, :], in1=cw9[0:9, :], op=ALU.mult), R=["scf", "cw9"], W=["scw"])
            orow = {"n": 4, "k": k_s[i * 4:(i + 1) * 4, :], "v": v_s[i * 4:(i + 1) * 4, :], "lf": lf_s[i * 4:(i + 1) * 4, :],
                    "mixed": mixed_s[i][:, 512:1024], "conv": conv_s[i], "convrows": (1, 4)}
            mix_tile(ctx, "S", xsd[i * 4:(i + 1) * 4, :], NPG, 0, cst[:, 513:514], True, orow)
            state_out(ssm_s[i])
            s_block(trs, "trs", vaug, "vaug", rkn[:], "rkn", True, True, False)
            for jj in range(NPG):
                n = base + jj
                if PIPE:
                    if jj + 1 < NPG:
                        ensure_gather(n + 2)
                        prep(n + 1)
                else:
                    ensure_gather(n + 1)
                    prep(n)
                sl_ = n % 2
                s_block(kTp[sl_], "kTp%d" % sl_, vbp[sl_], "vbp%d" % sl_, rkp[sl_][:], "rkp%d" % sl_, False, False, jj == NPG - 1)
            for hh in range(2):
                a3 = F[2 + hh][0:32, 0:260].rearrange("p (h d) -> p h d", h=4)
                b.op("dve", lambda e, hh=hh, a3=a3: e.reciprocal(out=rl8[0:32, 4 * hh:4 * hh + 4], in_=a3[:, :, 64]), R=[FK[2 + hh]], W=["rl8"])
                b.op("dve", lambda e, hh=hh, a3=a3: e.tensor_tensor(out=ot[0:32, 4 * hh:4 * hh + 4, :], in0=a3[:, :, 0:64],
                                                                   in1=rl8[0:32, 4 * hh:4 * hh + 4].unsqueeze(2).broadcast_to([32, 4, 64]), op=ALU.mult), R=[FK[2 + hh], "rl8"], W=["ot"])
            for h in range(8):
                ld("sp", mixed_s[i][0:4, h * 64:(h + 1) * 64], ot[4 * h:4 * h + 4, h, :], ["s%dmixed_att" % i], "d_ot%d" % h, R=["ot"])
        b.barrier()

    with contextlib.ExitStack() as s2:
        KA = [b.sb("KA%d" % i, [P, NB_P * P], BF16, s2) for i in range(2)]
        VA = [b.sb("VA%d" % i, [P, NB_P, 65], BF16, s2) for i in range(2)]
        QA = [b.sb("QA%d" % i, [P, NOWN * P], BF16, s2) for i in range(2)]
        Pt = [b.sb("Pt%d" % i, [P, 512], BF16, s2) for i in range(3)]
        rl = b.sb("rl", [P, 4], F32, s2)
        att = b.sb("att", [P, 4, 64], BF16, s2)
        for i in range(2):
            b.op("pool", lambda e, i=i: e.memset(KA[i][:], 0.0), W=["KA%d" % i])
            b.op("pool", lambda e, i=i: e.memset(QA[i][:], 0.0), W=["QA%d" % i])
            b.op("pool", lambda e, i=i: e.memset(KA[i][64:67, :], 1.0), W=["KA%d" % i])
        hc = [0]
        pc = [0]

        def attend(ctx, nblk, nq_tiles, nqc, negc_fn, negk, mixed_fn):
            npre = nblk - nq_tiles
            for h in range(8):
                s = hc[0] % 2
                hc[0] += 1
                ka, va, qa = KA[s], VA[s], QA[s]
                kk, vk, qkx = "KA%d" % s, "VA%d" % s, "QA%d" % s
                ld("sp", ka[0:64, 0:nblk * P], ctx["kT"][h * 64:(h + 1) * 64, :], [kk], "d_" + kk, R=[ctx["name"] + "kT"])
                vsrc_all = ctx["va"].rearrange("b p (h d) -> p b h d", h=8)
                for b0 in range(0, nblk, 16):
                    b1 = min(nblk, b0 + 16)
                    ld("sp", va[:, b0:b1, :], vsrc_all[:, b0:b1, h, :], [vk], "d_%s_%d" % (vk, (b0 // 16) % 4), R=[ctx["name"] + "va"])
                ld("sp", qa[0:64, 0:nq_tiles * P], ctx["qT"][h * 64:(h + 1) * 64, :], [qkx], "d_" + qkx, R=[ctx["name"] + "qT"])
                ld("sp", qa[64:67, 0:nq_tiles * P], ctx["qa"].rearrange("(j h) t -> h j t", h=8)[h], [qkx], "d_" + qkx + "b", R=[ctx["name"] + "qa"])
                for G in range(0, nq_tiles, 4):
                    ng = min(4, nq_tiles - G)
                    W_ = (ng - 1) * P + nqc
                    last_j = npre + G + ng - 1
                    def qk_step(j, slot):
                        r = j - (npre + G)
                        c0 = 0 if r < 0 else r * P
                        sc, sck = F[slot % 2], FK[slot % 2]
                        diag = r >= 0
                        b.op("pe", lambda e: e.matmul(sc[:, c0:W_], lhsT=ka[0:67, j * P:(j + 1) * P], rhs=qa[0:67, G * P + c0:G * P + W_], start=True, stop=not diag),
                             R=[kk, qkx], W=[sck], inc=not diag)
                        if diag:
                            wd_ = min(P, W_ - c0)
                            b.op("pe", lambda e: e.matmul(sc[:, c0:c0 + wd_], lhsT=identb[:], rhs=negmb[:, 0:wd_], start=False, stop=True),
                                 R=["identb", "negmb"], W=[sck])

                    def ex_pv_step(j, slot):
                        r = j - (npre + G)
                        c0 = 0 if r < 0 else r * P
                        sc, sck = F[slot % 2], FK[slot % 2]
                        pt_, ptk = Pt[slot % 3], "Pt%d" % (slot % 3)
                        b.op("act", lambda e: e.activation(out=pt_[:, c0:W_], in_=sc[:, c0:W_], func=AF.Exp, bias=negc_fn(j)[:, h:h + 1]),
                             R=[sck, negk], W=[ptk])
                        for qbk in range(max(r, 0), ng):
                            w_ = nqc if qbk == ng - 1 else P
                            w_ = min(w_, W_ - qbk * P)
                            b.op("pe", lambda e, qbk=qbk, w_=w_: e.matmul(F[2 + qbk][0:w_, 0:65], lhsT=pt_[:, qbk * P:qbk * P + w_], rhs=va[:, j, :],
                                                                   start=(j == 0), stop=(j == npre + G + qbk)),
                                 R=[ptk, vk], W=[FK[2 + qbk]], inc=True)

                    nsteps = last_j + 1
                    qk_step(0, pc[0])
                    for j in range(nsteps):
                        if j + 1 < nsteps:
                            qk_step(j + 1, pc[0] + j + 1)
                        ex_pv_step(j, pc[0] + j)
                    pc[0] += nsteps
                    for qbk in range(ng):
                        b.op("dve", lambda e, qbk=qbk: e.reciprocal(out=rl[:, qbk:qbk + 1], in_=F[2 + qbk][:, 64:65]), R=[FK[2 + qbk]], W=["rl"])
                        b.op("dve", lambda e, qbk=qbk: e.tensor_scalar(out=att[:, qbk, :], in0=F[2 + qbk][:, 0:64], scalar1=rl[:, qbk:qbk + 1], scalar2=None, op0=ALU.mult), R=[FK[2 + qbk], "rl"], W=["att"])
                    ld("pool", mixed_fn(G, ng, h), att[:, 0:ng, :], [ctx["name"] + "mixed_att"], "d_att", R=["att"])

        for i_ in range(2, 6):
            b.op("dve", lambda e, i_=i_: e.memset(F[i_][:, :], 1.0), W=[FK[i_]])
        attend(ctxp, NB_P, NOWN, P, lambda j: negc_p[:, j, :], "negc_p",
               lambda G, ng, h: mixed_p[G * P:(G + ng) * P, h * 64:(h + 1) * 64].rearrange("(q p) d -> p q d", p=P))
        b.barrier()

    with contextlib.ExitStack() as s3:
        woutb = b.sb("woutb", [P, 8, 1024], BF16, s3)
        wcqb = b.sb("wcqb", [P, 8, 1024], BF16, s3)
        wcob = b.sb("wcob", [P, 8, 1024], BF16, s3)
        wckb = b.sb("wckb", [P, 8, 1024], BF16, s3)
        wcvb = b.sb("wcvb", [P, 8, 1024], BF16, s3)
        gout = b.sb("gout", [P, 8], F32, s3)
        b.op("dve", lambda e: e.memset(gout[:], 1.0), W=["gcol2"])
        b.op("dve", lambda e: e.tensor_copy(out=gout[:, 4:8], in_=gcol[:, 32:36]), R=["gcol"], W=["gcol2"])
        with contextlib.ExitStack() as s3a:
            stage = [b.sb("stage%d" % i, [P, 8 * 512], F32, s3a) for i in range(2)]
            for (dst, wd, gc, key) in ((woutb, w_out, gout[:, 0:8], "woutb"), (wcqb, w_cq, gcol[:, 8:16], "wcqb"), (wcob, w_co, None, "wcob"),
                                       (wckb, w_ck, gcol[:, 16:24], "wckb"), (wcvb, w_cv, gcol[:, 16:24], "wcvb")):
                for c0 in (0, 512):
                    load_weight(stage, dst, wd, 8, c0, c0 + 512, gc=gc, dst_off=c0, key=key)
            b.barrier()
        xt = [b.sb("xt%d" % i, [P, 1024], F32, s3) for i in range(2)]
        mxt = [b.sb("mxt%d" % i, [P, 1024], BF16, s3) for i in range(2)]
        junk = b.sb("junk", [P, 1024], F32, s3)
        ms = b.sb("ms", [P, 4], F32, s3)
        xn = b.sb("xn", [P, 1024], BF16, s3)
        hT = b.sb("hT", [P, 8, P], BF16, s3)
        x1 = b.sb("x1", [P, 1024], F32, s3)
        q2T = b.sb("q2T", [P, 8, P], BF16, s3)
        mkT = [b.sb("mkT%d" % i, [P, 8, 256], BF16, s3) for i in range(2)]
        mvb = [b.sb("mvb%d" % i, [P, 2, 1024], BF16, s3) for i in range(2)]
        mkb = b.sb("mkb", [P, 1024], BF16, s3)
        smax = b.sb("smax", [P, 4], F32, s3)
        ssum = b.sb("ssum", [P, 4], F32, s3)
        pexp = b.sb("pexp", [P, 4, 256], BF16, s3)
        pT = b.sb("pT", [P, 8, P], BF16, s3)
        ob = b.sb("ob", [P, 1024], BF16, s3)
        x2 = b.sb("x2", [P, 1024], F32, s3)
        for t_, k_ in ((xt[0], "xt0"), (xt[1], "xt1"), (mxt[0], "mxt0"), (mxt[1], "mxt1")):
            b.op("pool", lambda e, t_=t_: e.memset(t_[:], 0.0), W=[k_])

        def norm_T(src, skey, dstT, dkey):
            rms_rstd(src, 1024, junk[:], ms, skey, "junk", "ms")
            b.op("act", lambda e: e.activation(out=xn[:], in_=src, func=AF.Copy, scale=ms[:, 0:1]), R=[skey, "ms"], W=["xn"])
            tv0 = T[0][:].rearrange("p (c t) -> p c t", c=8)
            transposes(tv0, "T0", xn, 8, "xn")
            b.op("dve", lambda e: e.tensor_copy(out=dstT[:], in_=tv0), R=["T0"], W=[dkey])

        def build_mem(slot, ksrc_fn, vsrc_fn):
            for mt in range(2):
                kap, kkey = ksrc_fn(mt)
                b.op("act", lambda e: e.copy(out=mkb[:], in_=kap), R=[kkey], W=["mkb"])
                tv = T[1][:].rearrange("p (c t) -> p c t", c=8)
                transposes(tv, "T1", mkb, 8, "mkb")
                b.op("dve", lambda e, mt=mt: e.tensor_copy(out=mkT[slot][:, :, mt * P:(mt + 1) * P], in_=tv), R=["T1"], W=["mkT%d" % slot])
                vap, vkey = vsrc_fn(mt)
                b.op("dve", lambda e, mt=mt: e.tensor_copy(out=mvb[slot][:, mt, :], in_=vap), R=[vkey], W=["mvb%d" % slot])

        memk_sb = [b.sb("memk_sb%d" % i, [P, 1024], F32, s3) for i in range(2)]
        memv_sb = [b.sb("memv_sb%d" % i, [P, 1024], F32, s3) for i in range(2)]
        for mt in range(2):
            ld("sp", xt[0][:], memd[mt * P:(mt + 1) * P, :], ["xt0"], "d_xt0")
            norm_T(xt[0][:], "xt0", hT, "hT")
            for (wb, wk, dst, dk, od) in ((wckb, "wckb", memk_sb, "memk_sb%d" % mt, memk_o), (wcvb, "wcvb", memv_sb, "memv_sb%d" % mt, memv_o)):
                for hb in range(2):
                    mm_group(F[hb][:, :], FK[hb], [(hT[:, c, :], wb[:, c, hb * 512:(hb + 1) * 512]) for c in range(8)], ["hT", wk])
                    b.op("act", lambda e, hb=hb, dst=dst: e.copy(out=dst[mt][:, hb * 512:(hb + 1) * 512], in_=F[hb][:, :]), R=[FK[hb]], W=[dk])
                ld("pool", od[mt * P:(mt + 1) * P, :], dst[mt][:], [], "d_" + dk, R=[dk])
        build_mem(0, lambda mt: (memk_sb[mt][:], "memk_sb%d" % mt), lambda mt: (memv_sb[mt][:], "memv_sb%d" % mt))

        tl = [0]

        def row_tile(x_src, x_rows, mixed_src, mslot, x2_dst, nrows):
            s = tl[0] % 2
            tl[0] += 1
            xk, mk_ = "xt%d" % s, "mxt%d" % s
            ld("sp", xt[s][0:x_rows, :], x_src, [xk], "d_" + xk)
            ld("sp", mxt[s][0:x_rows, :], mixed_src, [mk_], "d_" + mk_, R=["mixedsrc"])
            tv0 = T[0][:].rearrange("p (c t) -> p c t", c=8)
            transposes(tv0, "T0", mxt[s], 8, mk_)
            b.op("dve", lambda e: e.tensor_copy(out=hT[:], in_=tv0), R=["T0"], W=["hT"])
            for hb in range(2):
                mm_group(F[hb][:, :], FK[hb], [(hT[:, c, :], woutb[:, c, hb * 512:(hb + 1) * 512]) for c in range(8)], ["hT", "woutb"])
                b.op("dve", lambda e, hb=hb: e.tensor_tensor(out=x1[:, hb * 512:(hb + 1) * 512], in0=F[hb][:, :], in1=xt[s][:, hb * 512:(hb + 1) * 512], op=ALU.add), R=[FK[hb], xk], W=["x1"])
            norm_T(x1[:], "x1", hT, "hT")
            for half in range(2):
                for cc in range(4):
                    ec = half * 4 + cc
                    mm_group(F[2 + half][:, cc * P:(cc + 1) * P], FK[2 + half], [(wcqb[:, c, ec * P:(ec + 1) * P], hT[:, c, :]) for c in range(8)], ["hT", "wcqb"])
                b.op("act", lambda e, half=half: e.activation(out=q2T[:, half * 4:(half + 1) * 4, :].rearrange("p c t -> p (c t)"), in_=F[2 + half][:, :], func=AF.Copy, scale=X_SCALE), R=[FK[2 + half]], W=["q2T"])
            for hp in range(2):
                for hh in range(2):
                    h = hp * 2 + hh
                    mm_group(F[hp][:, hh * 256:(hh + 1) * 256], FK[hp], [(q2T[:, 2 * h + cc, :], mkT[mslot][:, 2 * h + cc, :]) for cc in range(2)], ["q2T", "mkT%d" % mslot])
                b.op("dve", lambda e, hp=hp: e.tensor_reduce(out=smax[:, hp * 2:hp * 2 + 2], in_=F[hp][:, :].rearrange("p (h m) -> p h m", h=2), axis=AX.X, op=ALU.max), R=[FK[hp]], W=["smax"])
            b.op("dve", lambda e: e.tensor_scalar(out=smax[:], in0=smax[:], scalar1=-1.0, scalar2=None, op0=ALU.mult), R=["smax"], W=["smax"])
            for h in range(4):
                b.op("act", lambda e, h=h: e.activation(out=pexp[:, h, :], in_=F[h // 2][:, (h % 2) * 256:(h % 2 + 1) * 256], func=AF.Exp, bias=smax[:, h:h + 1], accum_out=ssum[:, h:h + 1]),
                     R=[FK[h // 2], "smax"], W=["pexp", "ssum"])
            tv1 = T[1][:].rearrange("p (c t) -> p c t", c=8)
            for h in range(4):
                for mt in range(2):
                    b.op("pe", lambda e, h=h, mt=mt: e.transpose(out=tv1[:, h * 2 + mt, :], in_=pexp[:, h, mt * P:(mt + 1) * P], identity=identb[:]), R=["pexp", "identb"], W=["T1"], inc=(h == 3 and mt == 1))
            b.op("dve", lambda e: e.tensor_copy(out=pT[:], in_=tv1), R=["T1"], W=["pT"])
            b.op("dve", lambda e: e.reciprocal(out=ssum[:], in_=ssum[:]), R=["ssum"], W=["ssum"])
            for hp in range(2):
                for hh in range(2):
                    h = hp * 2 + hh
                    mm_group(F[2 + hp][:, hh * 256:(hh + 1) * 256], FK[2 + hp], [(pT[:, h * 2 + mt, :], mvb[mslot][:, mt, h * 256:(h + 1) * 256]) for mt in range(2)], ["pT", "mvb%d" % mslot])
                b.op("dve", lambda e, hp=hp: e.tensor_tensor(out=ob[:, hp * 512:(hp + 1) * 512].rearrange("p (h d) -> p h d", h=2), in0=F[2 + hp][:, :].rearrange("p (h d) -> p h d", h=2),
                                                          in1=ssum[:, hp * 2:hp * 2 + 2].unsqueeze(2).broadcast_to([P, 2, 256]), op=ALU.mult), R=[FK[2 + hp], "ssum"], W=["ob"])
            tv0 = T[0][:].rearrange("p (c t) -> p c t", c=8)
            transposes(tv0, "T0", ob, 8, "ob")
            b.op("dve", lambda e: e.tensor_copy(out=hT[:], in_=tv0), R=["T0"], W=["hT"])
            for hb in range(2):
                mm_group(F[hb][:, :], FK[hb], [(hT[:, c, :], wcob[:, c, hb * 512:(hb + 1) * 512]) for c in range(8)], ["hT", "wcob"])
                b.op("dve", lambda e, hb=hb: e.tensor_tensor(out=x2[:, hb * 512:(hb + 1) * 512], in0=F[hb][:, :], in1=x1[:, hb * 512:(hb + 1) * 512], op=ALU.add), R=[FK[hb], "x1"], W=["x2"])
            ld("pool", x2_dst, x2[0:nrows, :], ["x2dst"], "d_x2", R=["x2"])

        for t in range(NOWN):
            row_tile(xo[t * P:(t + 1) * P, :], P, mixed_p[t * P:(t + 1) * P, :], 0, x2_p[t * P:(t + 1) * P, :], P)
        b.op("dve", lambda e: e.memset(xt[0][:], 0.0), W=["xt0"])
        b.op("dve", lambda e: e.memset(xt[1][:], 0.0), W=["xt1"])
        cmk_sb = [b.sb("cmk_sb%d" % i, [P, 1024], F32, s3) for i in range(2)]
        cmv_sb = [b.sb("cmv_sb%d" % i, [P, 1024], F32, s3) for i in range(2)]
        for i in range(NSEQ):
            for mt in range(2):
                ld("sp", cmk_sb[mt][:], cmkd[i, mt * P:(mt + 1) * P, :], ["cmk_sb%d" % mt], "d_cmk%d" % mt)
                ld("sp", cmv_sb[mt][:], cmvd[i, mt * P:(mt + 1) * P, :], ["cmv_sb%d" % mt], "d_cmv%d" % mt)
            build_mem(1, lambda mt: (cmk_sb[mt][:], "cmk_sb%d" % mt), lambda mt: (cmv_sb[mt][:], "cmv_sb%d" % mt))
            row_tile(xsd[i * 4:(i + 1) * 4, :], 4, mixed_s[i][0:4, :], 1, x2_s[i * 4:(i + 1) * 4, :], 4)
        b.barrier()

    with contextlib.ExitStack() as s4:
        wgb = b.sb("wgb", [P, 8, DFF], BF16, s4)
        wub = b.sb("wub", [P, 8, DFF], BF16, s4)
        wdb = b.sb("wdb", [P, 22, 1024], BF16, s4)
        with contextlib.ExitStack() as s4a:
            stage = [b.sb("stage%d" % i, [P, 22 * 256], F32, s4a) for i in range(2)]
            for (dst, wd, key) in ((wgb, w_gate, "wgb"), (wub, w_up, "wub")):
                for c0 in range(0, DFF, 512):
                    c1 = min(c0 + 512, DFF)
                    load_weight(stage, dst, wd, 8, c0, c1, gc=gcol[:, 24:32], dst_off=c0, key=key)
            for c0 in range(0, 1024, 256):
                load_weight(stage, wdb, w_down, 22, c0, c0 + 256, gc=None, dst_off=c0, key="wdb")
            b.barrier()
        xt = [b.sb("xt%d" % i, [P, 1024], F32, s4) for i in range(2)]
        junk = b.sb("junk", [P, 1024], F32, s4)
        ms = b.sb("ms", [P, 4], F32, s4)
        xn = b.sb("xn", [P, 1024], BF16, s4)
        hT = b.sb("hT", [P, 8, P], BF16, s4)
        gs = b.sb("gs", [P, 512], F32, s4)
        ub = b.sb("ub", [P, DFF], BF16, s4)
        uT = b.sb("uT", [P, 22, P], BF16, s4)
        x3 = b.sb("x3", [P, 1024], F32, s4)
        yo = [b.sb("yo%d" % i, [P, 1024], F32, s4) for i in range(2)]
        gfin = b.sb("gfin", [P, 1024], F32, s4)
        ld("sp", gfin[:], gfind.partition_broadcast(P), ["gfin"], "d_gfin")
        b.op("pool", lambda e: e.memset(xt[0][:], 0.0), W=["xt0"])
        b.op("pool", lambda e: e.memset(xt[1][:], 0.0), W=["xt1"])
        tl4 = [0]

        def ffn_tile(x_src, nrows, y_dst):
            s = tl4[0] % 2
            tl4[0] += 1
            xk = "xt%d" % s
            ld("sp", xt[s][0:nrows, :], x_src, [xk], "d_" + xk, R=["x2dst"])
            rms_rstd(xt[s][:], 1024, junk[:], ms, xk, "junk", "ms")
            b.op("act", lambda e: e.activation(out=xn[:], in_=xt[s][:], func=AF.Copy, scale=ms[:, 0:1]), R=[xk, "ms"], W=["xn"])
            tv0 = T[0][:].rearrange("p (c t) -> p c t", c=8)
            transposes(tv0, "T0", xn, 8, "xn")
            b.op("dve", lambda e: e.tensor_copy(out=hT[:], in_=tv0), R=["T0"], W=["hT"])
            nb = (DFF + 511) // 512
            for bi in range(nb):
                c0 = bi * 512
                c1 = min(c0 + 512, DFF)
                w = c1 - c0
                mm_group(F[0][:, 0:w], "F0", [(hT[:, c, :], wgb[:, c, c0:c1]) for c in range(8)], ["hT", "wgb"])
                mm_group(F[1][:, 0:w], "F1", [(hT[:, c, :], wub[:, c, c0:c1]) for c in range(8)], ["hT", "wub"])
                b.op("act", lambda e, w=w: e.activation(out=gs[:, 0:w], in_=F[0][:, 0:w], func=AF.Silu), R=["F0"], W=["gs"])
                b.op("dve", lambda e, w=w, c0=c0, c1=c1: e.tensor_tensor(out=ub[:, c0:c1], in0=gs[:, 0:w], in1=F[1][:, 0:w], op=ALU.mult), R=["gs", "F1"], W=["ub"])
            for t8 in range(0, 22, 8):
                n8 = min(8, 22 - t8)
                tv = T[(t8 // 8) % 2][:].rearrange("p (c t) -> p c t", c=8)
                tk_ = TK[(t8 // 8) % 2]
                for c in range(n8):
                    b.op("pe", lambda e, c=c, t8=t8, tv=tv: e.transpose(out=tv[:, c, :], in_=ub[:, (t8 + c) * P:(t8 + c + 1) * P], identity=identb[:]), R=["ub", "identb"], W=[tk_], inc=(c == n8 - 1))
                b.op("dve", lambda e, t8=t8, n8=n8, tv=tv: e.tensor_copy(out=uT[:, t8:t8 + n8, :], in_=tv[:, 0:n8, :]), R=[tk_], W=["uT"])
            for hb in range(2):
                mm_group(F[2 + hb][:, :], FK[2 + hb], [(uT[:, c, :], wdb[:, c, hb * 512:(hb + 1) * 512]) for c in range(22)], ["uT", "wdb"])
                b.op("dve", lambda e, hb=hb: e.tensor_tensor(out=x3[:, hb * 512:(hb + 1) * 512], in0=F[2 + hb][:, :], in1=xt[s][:, hb * 512:(hb + 1) * 512], op=ALU.add), R=[FK[2 + hb], xk], W=["x3"])
            rms_rstd(x3[:], 1024, junk[:], ms[:, 1:2], "x3", "junk", "ms")
            yk = "yo%d" % s
            b.op("dve", lambda e: e.scalar_tensor_tensor(out=yo[s][:], in0=x3[:], scalar=ms[:, 1:2], in1=gfin[:], op0=ALU.mult, op1=ALU.mult), R=["x3", "ms", "gfin"], W=[yk])
            ld("pool", y_dst, yo[s][0:nrows, :], [], "d_" + yk, R=[yk])

        for t in range(NOWN):
            ffn_tile(x2_p[t * P:(t + 1) * P, :], P, y_o[t * P:(t + 1) * P, :])
        ffn_tile(x2_s[0:NSEQ * 4, :], NSEQ * 4, y_s)
        b.barrier()

    b.es.close()
    return nc


_NC_CACHE = {}


def make_shared(inp):
    f = np.float32
    P = 128
    cst = np.zeros((P, 1024), f)
    ii = np.arange(P)
    cst[:, 0:128] = np.eye(P, dtype=f)
    cst[:, 128:256] = (ii[:, None] <= ii[None, :]).astype(f)
    cst[:, 256:384] = (ii[:, None] > ii[None, :]).astype(f)
    cst[:, 384:512] = np.where(ii[:, None] > ii[None, :], -30000.0, 0.0).astype(f)
    cst[:, 512] = ii.astype(f)
    cst[0:4, 513] = 1.0
    for j in range(3):
        for h in range(8):
            cst[j * 8 + h, 520 + h] = 1.0
    for col in range(32):
        cst[:, 528 + col] = np.where(ii > (col % 4), -30000.0, 0.0)
    for k in range(3):
        for j in range(3):
            t = k - j
            if 0 <= t < 4:
                cst[3 * k + j, 896 + t] = 1.0

    def col8(g):
        return np.ascontiguousarray(np.asarray(g, f).reshape(-1, P).T)

    gcolv = np.zeros((P, 40), f)
    gcolv[:, 0:8] = col8(inp["norm_mix_g"][0])
    gcolv[:, 8:16] = col8(inp["norm_cross_g"][0])
    gcolv[:, 16:24] = col8(inp["norm_mem_g"][0])
    gcolv[:, 24:32] = col8(inp["norm_ffn_g"][0])
    gcolv[:, 32:36] = col8(inp["ssm_norm_g"][0])
    rvs = np.concatenate([np.asarray(inp[k][0], f) for k in ("b_forget", "dt_bias", "a_log", "d_skip")]).reshape(1, 32)
    nphys = np.asarray(inp["cache_k"]).shape[1]
    shared = {
        "ckv": np.concatenate([np.asarray(inp["cache_k"], f).reshape(nphys * 128, 512),
                               np.asarray(inp["cache_v"], f).reshape(nphys * 128, 512),
                               np.asarray(inp["cache_logf"], f).reshape(nphys * 128, 8)], axis=1),
        "gcol": gcolv, "rvs": rvs, "convb": np.asarray(inp["conv_b"][0], f).reshape(1, 1024),
        "convw": np.asarray(inp["conv_w"][0], f).reshape(1, 4096), "gfin": np.asarray(inp["final_norm_g"], f).reshape(1, 1024),
        "cst": cst,
    }
    for k in ("w_in", "w_out", "w_cq", "w_ck", "w_cv", "w_co", "w_gate", "w_up", "w_down"):
        shared[k] = np.asarray(inp[k][0], f)
    return shared


def make_core_map(shared, inp, cfg, sq, own_start, s0):
    f = np.float32
    P = 128
    NPRE, NOWN, NSEQ, NPG = cfg["NPRE"], cfg["NOWN"], cfg["NSEQ"], cfg["NPG"]
    x_prompt = np.asarray(inp["x_prompt"], f)
    xpre = np.zeros((NPRE * P, 1024), f)
    npre_tok = min(own_start, NPRE * P)
    assert npre_tok == own_start
    if npre_tok:
        xpre[NPRE * P - npre_tok:] = x_prompt[sq, own_start - npre_tok:own_start]
    valid = np.zeros((P, NPRE), f)
    if npre_tok:
        valid[:, NPRE - npre_tok // P:] = 1.0
    sl_ = slice(s0, s0 + NSEQ)
    m = dict(shared)
    m.update({
        "xo": np.ascontiguousarray(x_prompt[sq, own_start:own_start + NOWN * P]),
        "xpre": xpre, "valid": valid,
        "xs": np.ascontiguousarray(np.asarray(inp["x_sample"], f)[sl_].reshape(NSEQ * 4, 1024)),
        "mem": np.ascontiguousarray(np.asarray(inp["mem_prompt"], f)[sq]),
        "pt": np.ascontiguousarray(np.asarray(inp["page_table"], np.int32)[sl_].reshape(1, NSEQ * NPG)),
        "cmk": np.ascontiguousarray(np.asarray(inp["cache_mem_k"], f)[0, sl_].reshape(NSEQ, 256, 1024)),
        "cmv": np.ascontiguousarray(np.asarray(inp["cache_mem_v"], f)[0, sl_].reshape(NSEQ, 256, 1024)),
        "sconv": np.ascontiguousarray(np.asarray(inp["state_conv"], f)[0, sl_]),
        "sssm": np.ascontiguousarray(np.asarray(inp["state_ssm"], f)[0, sl_].reshape(NSEQ, 512, 128)),
    })
    return m


def kernel(**inp):
    f = np.float32
    cfg = FULL_CFG
    NSEQ = cfg["NSEQ"]
    shared = make_shared(inp)
    in_maps = [make_core_map(shared, inp, cfg, c // 4, (c % 4) * 2048, c * NSEQ) for c in range(8)]
    if "nc" not in _NC_CACHE:
        _NC_CACHE["nc"] = build(cfg)
    res = run_bass_kernel_spmd(_NC_CACHE["nc"], in_maps, core_ids=list(range(8)))
    R = res.results

    def cat(name, shape):
        return np.concatenate([np.asarray(R[c][name], f) for c in range(8)], axis=0).reshape(shape)

    y_prompt = cat("y_o", (2, 8192, 1024))
    k_prompt = cat("k_o", (1, 2, 8192, 8, 64))
    v_prompt = cat("v_o", (1, 2, 8192, 8, 64))
    lf_prompt = cat("lf_o", (1, 2, 8192, 8))
    memk = np.stack([np.asarray(R[0]["memk_o"], f), np.asarray(R[4]["memk_o"], f)]).reshape(1, 2, 256, 4, 256)
    memv = np.stack([np.asarray(R[0]["memv_o"], f), np.asarray(R[4]["memv_o"], f)]).reshape(1, 2, 256, 4, 256)
    conv_p = np.stack([np.asarray(R[3]["conv_o"], f), np.asarray(R[7]["conv_o"], f)]).reshape(1, 2, 3, 1024)
    ssm_p = np.stack([np.asarray(R[3]["ssm_o"], f), np.asarray(R[7]["ssm_o"], f)]).reshape(1, 2, 8, 64, 128)
    y_sample = cat("y_s", (128, 4, 1024))
    k_sample = cat("k_s", (1, 128, 4, 8, 64))
    v_sample = cat("v_s", (1, 128, 4, 8, 64))
    lf_sample = cat("lf_s", (1, 128, 4, 8))
    conv_sm = cat("conv_s", (1, 128, 3, 1024))
    ssm_sm = cat("ssm_s", (1, 128, 8, 64, 128))
    return (y_prompt, y_sample, k_prompt, v_prompt, lf_prompt, memk, memv, conv_p, ssm_p,
            k_sample, v_sample, lf_sample, conv_sm, ssm_sm)
```

```python
import contextlib
import numpy as np
import concourse.bass as bass
import concourse.mybir as mybir
from concourse.bass_utils import run_bass_kernel_spmd

F32 = mybir.dt.float32
BF16 = mybir.dt.bfloat16
I32 = mybir.dt.int32
AF = mybir.ActivationFunctionType
ALU = mybir.AluOpType
AX = mybir.AxisListType

import os
PIPE = int(os.environ.get("K_PIPE", "1"))
STOP = int(os.environ.get("K_STOP", "0"))
STOPT = int(os.environ.get("K_STOPT", "999"))
STOPO = int(os.environ.get("K_STOPO", "999"))


class _Stop(Exception):
    pass


def finish(b):
    b.barrier()
    raise _Stop()

FULL_CFG = dict(NPRE=48, NOWN=16, NSEQ=16, NPG=16, NPHYS=2560)
EPS = 1e-6
ATT_SCALE = 64 ** -0.5
X_SCALE = 256 ** -0.5
DFF = 2816


class Builder:
    def __init__(self):
        self.nc = bass.Bass("TRN2", target_bir_lowering=False)
        self.es = contextlib.ExitStack()
        nc = self.nc
        self.eng = {"pe": nc.tensor, "act": nc.scalar, "dve": nc.vector, "pool": nc.gpsimd, "sp": nc.sync}
        self.semobj = {}
        self.cnt = {}
        for e in ("pe", "act", "dve", "pool"):
            self.semobj["s_" + e] = self.es.enter_context(nc.semaphore("s_" + e))
            self.cnt[e] = 0
        self.dcnt = {}
        self.tr = {}
        self.waited = {e: {} for e in self.eng}

    def sb(self, name, shape, dt, stack=None):
        self.uid = getattr(self, "uid", 0) + 1
        return (stack or self.es).enter_context(self.nc.sbuf_tensor("sb%d_%s" % (self.uid, name), shape, dt))

    def ps(self, name, shape, dt, stack=None):
        self.uid = getattr(self, "uid", 0) + 1
        return (stack or self.es).enter_context(self.nc.psum_tensor("ps%d_%s" % (self.uid, name), shape, dt))

    def dram(self, name, shape, dt, kind="Internal"):
        return self.nc.dram_tensor(name, shape, dt, kind=kind).ap()

    def _deps(self, R, W):
        d = {}

        def add(x):
            if x is not None:
                d[x[0]] = max(d.get(x[0], 0), x[1])
        for k in R:
            t = self.tr.get(k)
            if t:
                add(t[0])
        for k in W:
            t = self.tr.get(k)
            if t:
                add(t[0])
                for sn, v in t[1].items():
                    add((sn, v))
        return d

    def _wait(self, e, d, skip_self=False):
        for sn, v in d.items():
            if skip_self and sn == "s_" + e:
                continue
            if self.waited[e].get(sn, 0) >= v:
                continue
            self.eng[e].wait_ge(self.semobj[sn], v)
            self.waited[e][sn] = v

    def _mark(self, R, W, tag):
        for k in W:
            self.tr[k] = [tag, {}]
        for k in R:
            t = self.tr.setdefault(k, [None, {}])
            t[1][tag[0]] = max(t[1].get(tag[0], 0), tag[1])

    def op(self, e, fn, R=(), W=(), inc=True):
        d = self._deps(R, W)
        self._wait(e, d, skip_self=(e == "pe"))
        ins = fn(self.eng[e])
        if inc:
            self.cnt[e] += 1
            ins.then_inc(self.semobj["s_" + e], 1)
            tag = ("s_" + e, self.cnt[e])
        else:
            tag = ("s_" + e, self.cnt[e] + 1)
        self._mark(R, W, tag)

    def dma(self, q, out, in_, R=(), W=(), sem=None, **kw):
        d = self._deps(R, W)
        if sem not in self.semobj:
            self.semobj[sem] = self.es.enter_context(self.nc.semaphore(sem))
            self.dcnt[sem] = 0
        if self.dcnt[sem] > 0:
            d[sem] = max(d.get(sem, 0), self.dcnt[sem])
        self._wait(q, d)
        self.dcnt[sem] += 16
        self.eng[q].dma_start(out=out, in_=in_, **kw).then_inc(self.semobj[sem], 16)
        self._mark(R, W, (sem, self.dcnt[sem]))

    def gather(self, out, table, idx_ap, R=(), W=(), sem=None):
        d = self._deps(R, W)
        if sem not in self.semobj:
            self.semobj[sem] = self.es.enter_context(self.nc.semaphore(sem))
            self.dcnt[sem] = 0
        if self.dcnt[sem] > 0:
            d[sem] = max(d.get(sem, 0), self.dcnt[sem])
        self._wait("pool", d)
        self.dcnt[sem] += 16
        self.nc.gpsimd.indirect_dma_start(
            out=out, out_offset=None, in_=table,
            in_offset=bass.IndirectOffsetOnAxis(ap=idx_ap, axis=0),
        ).then_inc(self.semobj[sem], 16)
        self._mark(R, W, (sem, self.dcnt[sem]))

    def barrier(self):
        d = {"s_" + e: c for e, c in self.cnt.items() if c > 0}
        for s, c in self.dcnt.items():
            if c > 0:
                d[s] = c
        for e in self.eng:
            self._wait(e, dict(d))
        self.tr = {}


def _build(b, cfg):
    nc = b.nc
    P = 128
    NPRE, NOWN, NSEQ, NPG, NPHYS = cfg["NPRE"], cfg["NOWN"], cfg["NSEQ"], cfg["NPG"], cfg["NPHYS"]

    def din(name, shape, dt=F32):
        return b.dram(name, shape, dt, kind="ExternalInput")

    def dout(name, shape, dt=F32):
        return b.dram(name, shape, dt, kind="ExternalOutput")

    xo = din("xo", [NOWN * P, 1024])
    xpre = din("xpre", [NPRE * P, 1024])
    validd = din("valid", [P, NPRE])
    xsd = din("xs", [NSEQ * 4, 1024])
    memd = din("mem", [256, 1024])
    ckvd = din("ckv", [NPHYS * 128, 1032])
    ptd = din("pt", [1, NSEQ * NPG], I32)
    cmkd = din("cmk", [NSEQ, 256, 1024])
    cmvd = din("cmv", [NSEQ, 256, 1024])
    sconvd = din("sconv", [NSEQ, 3, 1024])
    sssmd = din("sssm", [NSEQ, 512, 128])
    w_in = din("w_in", [1024, 3088])
    w_out = din("w_out", [1024, 1024])
    w_cq = din("w_cq", [1024, 1024])
    w_ck = din("w_ck", [1024, 1024])
    w_cv = din("w_cv", [1024, 1024])
    w_co = din("w_co", [1024, 1024])
    w_gate = din("w_gate", [1024, DFF])
    w_up = din("w_up", [1024, DFF])
    w_down = din("w_down", [DFF, 1024])
    gcold = din("gcol", [P, 40])
    rvsd = din("rvs", [1, 32])
    convbd = din("convb", [1, 1024])
    convwd = din("convw", [1, 4096])
    gfind = din("gfin", [1, 1024])
    cstd = din("cst", [P, 1024])
    cst2d = din("cst2", [P, 6 * P])

    y_o = dout("y_o", [NOWN * P, 1024])
    k_o = dout("k_o", [NOWN * P, 512])
    v_o = dout("v_o", [NOWN * P, 512])
    lf_o = dout("lf_o", [NOWN * P, 8])
    memk_o = dout("memk_o", [256, 1024])
    memv_o = dout("memv_o", [256, 1024])
    conv_o = dout("conv_o", [3, 1024])
    ssm_o = dout("ssm_o", [512, 128])
    y_s = dout("y_s", [NSEQ * 4, 1024])
    k_s = dout("k_s", [NSEQ * 4, 512])
    v_s = dout("v_s", [NSEQ * 4, 512])
    lf_s = dout("lf_s", [NSEQ * 4, 8])
    conv_s = dout("conv_s", [NSEQ, 3, 1024])
    ssm_s = dout("ssm_s", [NSEQ, 512, 128])

    NB_P = NPRE + NOWN
    kT_p = b.dram("kT_p", [512, NB_P * P], BF16)
    va_p = b.dram("va_p", [NB_P, P, 8 * 65], BF16)
    qT_p = b.dram("qT_p", [512, NOWN * P], BF16)
    qa_p = b.dram("qa_p", [24, NOWN * P], BF16)
    mixed_p = b.dram("mixed_p", [NOWN * P, 1024], BF16)
    NB_S = NPG + 1
    mixed_s = b.dram("mixed_s", [NSEQ, P, 1024], BF16)
    x2_p = b.dram("x2_p", [NOWN * P, 1024], F32)
    x2_s = b.dram("x2_s", [P, 1024], F32)

    cst = b.sb("cst", [P, 1024], F32)
    identf = cst[:, 0:128]
    ut = cst[:, 128:256]
    sl = cst[:, 256:384]
    iota = cst[:, 512:513]
    identb = b.sb("identb", [P, P], BF16)
    negmb = b.sb("negmb", [P, P], BF16)
    onesf = b.sb("onesf", [P, P], F32)
    epsc = b.sb("epsc", [P, 1], F32)
    gcol = b.sb("gcol", [P, 40], F32)
    rvs = b.sb("rvs", [P, 32], F32)
    nexpA = b.sb("nexpA", [P, 8], F32)
    convb = b.sb("convb", [P, 1024], F32)
    validt = b.sb("validt", [P, NPRE], F32)
    negc_p = b.sb("negc_p", [P, NB_P, 8], F32)
    idx = b.sb("idx", [P, NSEQ * NPG], I32)

    F = [b.ps("F%d" % i, [P, 512], F32) for i in range(6)]
    T = [b.ps("T%d" % i, [P, 1024], BF16) for i in range(2)]
    FK = ["F%d" % i for i in range(6)]
    TK = ["T0", "T1"]

    def ld(q, out, in_, W, sem, R=()):
        b.dma(q, out, in_, R=R, W=W, sem=sem)

    ld("sp", cst[:], cstd, ["cst"], "d_cst")
    ld("sp", gcol[:], gcold, ["gcol"], "d_gcol")
    ld("sp", rvs[:], rvsd.partition_broadcast(P), ["rvs"], "d_rvs")
    ld("sp", convb[:], convbd.partition_broadcast(P), ["convb"], "d_convb")
    ld("sp", validt[:], validd, ["validt"], "d_valid")
    b.op("dve", lambda e: e.tensor_copy(out=identb[:], in_=identf), R=["cst"], W=["identb"])
    b.op("dve", lambda e: e.tensor_copy(out=negmb[:], in_=cst[:, 384:512]), R=["cst"], W=["negmb"])
    b.op("dve", lambda e: e.memset(onesf[:], 1.0), W=["onesf"])
    b.op("dve", lambda e: e.memset(epsc[:], EPS), W=["epsc"])
    b.op("act", lambda e: e.activation(out=nexpA[:], in_=rvs[:, 16:24], func=AF.Exp), R=["rvs"], W=["nexpA"])
    b.op("dve", lambda e: e.tensor_scalar(out=nexpA[:], in0=nexpA[:], scalar1=-1.0, scalar2=None, op0=ALU.mult), R=["nexpA"], W=["nexpA"])
    with contextlib.ExitStack() as st0:
        pti = b.sb("pti", [P, NSEQ * NPG], I32, st0)
        ptf = b.sb("ptf", [P, NSEQ * NPG], F32, st0)
        ld("sp", pti[:], ptd.partition_broadcast(P), ["pti"], "d_pt")
        b.op("dve", lambda e: e.tensor_copy(out=ptf[:], in_=pti[:]), R=["pti"], W=["ptf"])
        b.op("dve", lambda e: e.tensor_scalar(out=ptf[:], in0=ptf[:], scalar1=128.0, scalar2=iota, op0=ALU.mult, op1=ALU.add), R=["ptf", "cst"], W=["ptf"])
        b.op("dve", lambda e: e.tensor_copy(out=idx[:], in_=ptf[:]), R=["ptf"], W=["idx"])
        b.barrier()

    stage_cnt = [0]

    def load_weight(stk_stage, dst, wd, din_chunks, c0, c1, gc=None, dst_off=0, key=None):
        bw = c1 - c0
        s = stage_cnt[0] % 2
        stage_cnt[0] += 1
        stg = stk_stage[s]
        sk = "stage%d" % s
        view = stg[:, 0:din_chunks * bw].rearrange("p (c n) -> p c n", c=din_chunks)
        ld("sp", view, wd.rearrange("(c p) n -> p c n", p=P)[:, :, c0:c1], [sk], "d_" + sk)
        for c in range(din_chunks):
            if c % 2 == 0:
                if gc is not None:
                    b.op("dve", lambda e, c=c: e.tensor_scalar(out=dst[:, c, dst_off:dst_off + bw], in0=view[:, c, :], scalar1=gc[:, c:c + 1], scalar2=None, op0=ALU.mult), R=[sk, "gcol"], W=[key])
                else:
                    b.op("dve", lambda e, c=c: e.tensor_copy(out=dst[:, c, dst_off:dst_off + bw], in_=view[:, c, :]), R=[sk], W=[key])
            else:
                if gc is not None:
                    b.op("act", lambda e, c=c: e.activation(out=dst[:, c, dst_off:dst_off + bw], in_=view[:, c, :], func=AF.Copy, scale=gc[:, c:c + 1]), R=[sk, "gcol"], W=[key])
                else:
                    b.op("act", lambda e, c=c: e.copy(out=dst[:, c, dst_off:dst_off + bw], in_=view[:, c, :]), R=[sk], W=[key])
        return view, sk

    def rms_rstd(src, width, tmp, ms, key_src, key_tmp, key_ms):
        b.op("act", lambda e: e.activation(out=tmp, in_=src, func=AF.Square, scale=float(width) ** -0.5, accum_out=ms[:, 0:1]), R=[key_src], W=[key_tmp, key_ms])
        b.op("act", lambda e: e.activation(out=ms[:, 0:1], in_=ms[:, 0:1], func=AF.Sqrt, bias=epsc[:, 0:1]), R=[key_ms], W=[key_ms])
        b.op("dve", lambda e: e.reciprocal(out=ms[:, 0:1], in_=ms[:, 0:1]), R=[key_ms], W=[key_ms])

    def transposes(dstT, tk, src, nch, src_key, ident=None):
        for c in range(nch):
            b.op("pe", lambda e, c=c: e.transpose(out=dstT[:, c, :], in_=src[:, c * P:(c + 1) * P], identity=identb[:]),
                 R=[src_key, "identb"], W=[tk], inc=(c == nch - 1))

    def mm_group(out_ap, okey, pairs, R):
        n = len(pairs)
        for i, (l, r) in enumerate(pairs):
            b.op("pe", lambda e, l=l, r=r, i=i: e.matmul(out_ap, lhsT=l, rhs=r, start=(i == 0), stop=(i == n - 1)),
                 R=R, W=[okey], inc=(i == n - 1))

    with contextlib.ExitStack() as s1:
        winb = b.sb("winb", [P, 8, 3088], BF16, s1)
        wbc = b.sb("wbc", [P, 4, 1024], F32, s1)
        shm = b.sb("shm", [P, 6, P], BF16, s1)
        cur = [b.sb("cur%d" % i, [P, 4, 1024], BF16, s1) for i in range(2)]
        ld("sp", wbc[:].rearrange("p j n -> p (j n)"), convwd.partition_broadcast(P), ["wbc"], "d_wbc")
        with contextlib.ExitStack() as s1a:
            stage = [b.sb("stage%d" % i, [P, 8 * 512], F32, s1a) for i in range(2)]
            cst2 = b.sb("cst2", [P, 6 * P], F32, s1a)
            ld("sp", cst2[:], cst2d, ["cst2"], "d_cst2")
            b.op("dve", lambda e: e.tensor_copy(out=shm[:].rearrange("p j t -> p (j t)"), in_=cst2[:]), R=["cst2"], W=["shm"])
            for c0 in range(0, 3088, 512):
                c1 = min(c0 + 512, 3088)
                load_weight(stage, winb, w_in, 8, c0, c1, gc=gcol[:, 0:8], dst_off=c0, key="winb")
            b.barrier()

        xt = [b.sb("xt%d" % i, [P, 1024], F32, s1) for i in range(2)]
        ms = b.sb("ms", [P, 4], F32, s1)
        xn = b.sb("xn", [P, 1024], BF16, s1)
        hT = [b.sb("hT%d" % i, [P, 8, 131], BF16, s1) for i in range(2)]
        qb = b.sb("qb", [P, 512], BF16, s1)
        kf = b.sb("kf", [P, 512], F32, s1)
        kb = b.sb("kb", [P, 512], BF16, s1)
        vf = b.sb("vf", [P, 512], F32, s1)
        vaug = b.sb("vaug", [P, 8, 65], BF16, s1)
        mdiag = b.sb("mdiag", [P, 8], F32, s1)
        trs = b.sb("trs", [P, 4, P], BF16, s1)
        t16 = b.sb("t16", [P, 16], F32, s1)
        e16 = b.sb("e16", [P, 16], F32, s1)
        lf = b.sb("lf", [P, 8], F32, s1)
        dtv = b.sb("dtv", [P, 8], F32, s1)
        ncarry = b.sb("ncarry", [P, 8], F32, s1)
        cm = b.sb("cm", [P, 8], F32, s1)
        r1 = b.sb("r1", [P, 8], F32, s1)
        aug = b.sb("aug", [P, 128], BF16, s1)
        augT = b.sb("augT", [P, P], BF16, s1)
        sz = b.sb("sz", [P, 512], F32, s1)
        pre = b.sb("pre", [P, 512], F32, s1)
        xact = b.sb("xact", [P, 512], BF16, s1)
        bcact = b.sb("bcact", [P, 512], BF16, s1)
        av = b.sb("av", [P, 8], F32, s1)
        E24 = b.sb("E24", [P, 24], F32, s1)
        xdt = b.sb("xdt", [P, 8, 64], BF16, s1)
        xdd = b.sb("xdd", [P, 8, 64], BF16, s1)
        bcT = b.sb("bcT", [P, 4, P], BF16, s1)
        S = b.sb("S", [P, 512], F32, s1)
        Sb = b.sb("Sb", [P, 512], BF16, s1)
        Rm = b.sb("Rm", [P, 8, P], F32, s1)
        Rflat = Rm[:].rearrange("p h t -> p (h t)")
        qk = Rflat[:, 0:512]
        ytmp = Rflat[:, 512:1024]
        rawx = Rflat
        Dm = b.sb("Dm", [P, 8, P], BF16, s1)
        GTm = b.sb("GTm", [P, 2, P], F32, s1)
        Mm = b.sb("Mm", [P, 8, P], BF16, s1)
        ysb = b.sb("ysb", [P, 512], F32, s1)
        yn = b.sb("yn", [P, 512], BF16, s1)
        scw = b.sb("scw", [P, 1024], BF16, s1)
        scf = b.sb("scf", [P, 1024], F32, s1)
        cw9 = b.sb("cw9", [P, 1024], F32, s1)
        sh9 = b.sb("sh9", [P, P], BF16, s1)
        sst = b.sb("sst", [P, 4, P], F32, s1)
        pg = [b.sb("pg%d" % i, [P, 1032], F32, s1) for i in range(2)]
        kTp = [b.sb("kTp%d" % i, [P, 4, P], BF16, s1) for i in range(2)]
        vbp = [b.sb("vbp%d" % i, [P, 8, 65], BF16, s1) for i in range(2)]
        rkp = [b.sb("rkp%d" % i, [P, 8], F32, s1) for i in range(2)]
        Pp = [b.sb("Pp%d" % i, [P, 32], BF16, s1) for i in range(2)]
        sb32 = b.sb("sb32", [P, 32], F32, s1)
        rkn = b.sb("rkn", [P, 8], F32, s1)
        pcarry = b.sb("pcarry", [P, 8], F32, s1)
        Qbd = b.sb("Qbd", [P, 4, 8], BF16, s1)
        A24 = b.sb("A24", [P, 32], BF16, s1)
        negm32 = b.sb("negm32", [P, 32], BF16, s1)
        onesb = b.sb("onesb", [P, P], BF16, s1)
        rl8 = b.sb("rl8", [P, 8], F32, s1)
        ot = b.sb("ot", [P, 8, 64], BF16, s1)
        for t_ in (xn, aug, vaug, trs, Rm, scw, scf, cw9, sh9, yn, Qbd, A24, kTp[0], kTp[1], ot):
            b.op("pool", lambda e, t_=t_: e.memset(t_[:], 0.0), W=["init"])
        for t_ in (cur[0], cur[1], bcact):
            b.op("pool", lambda e, t_=t_: e.memset(t_[:], 0.0), W=["init"])
        for t_ in (vbp[0], vbp[1], onesb):
            b.op("pool", lambda e, t_=t_: e.memset(t_[:], 1.0), W=["init"])
        b.op("dve", lambda e: e.tensor_copy(out=negm32[:], in_=cst[:, 528:560]), R=["cst"], W=["init"])
        b.barrier()
        for k in range(3):
            ld("sp", cw9[3 * k:3 * k + 3, :], convwd[0:1, 0:3072].rearrange("o (j n) -> (o j) n", j=3), ["cw9"], "d_cw9")
        b.op("dve", lambda e: e.tensor_copy(out=sh9[0:9, :], in_=cst[0:9, 897:1025 - 0] if False else cst[0:9, 896:1024]), R=["cst"], W=["sh9"])
        b.barrier()

        tile_no = [0]

        def kv_post(ctx, blk, ksrc, ksrc_key, vsrc, vsrc_key, lf_ap, lf_key, vcol):
            b.op("act", lambda e: e.copy(out=kb[:], in_=ksrc), R=[ksrc_key], W=["kb"])
            tv = T[1][:, 0:512].rearrange("p (c t) -> p c t", c=4)
            transposes(tv, "T1", kb, 4, "kb")
            b.op("dve", lambda e: e.tensor_copy(out=trs[:], in_=tv), R=["T1"], W=["trs"])
            if not ctx.get("nostore"):
                ld("pool", ctx["kT"].rearrange("(c p) t -> p c t", p=P)[:, :, blk * P:(blk + 1) * P], trs[:], [ctx["name"] + "kT"], "d_trs", R=["trs"])
            b.op("dve", lambda e: e.tensor_copy(out=vaug[:, :, 0:64], in_=vsrc.rearrange("p (h d) -> p h d", h=8)), R=[vsrc_key], W=["vaug"])
            if vcol is None:
                b.op("dve", lambda e: e.memset(vaug[:, :, 64:65], 1.0), W=["vaug"])
            else:
                b.op("dve", lambda e: e.tensor_copy(out=vaug[:, :, 64:65], in_=vcol.unsqueeze(2).broadcast_to([P, 8, 1])), R=["validt"], W=["vaug"])
            if not ctx.get("nostore"):
                ld("pool", ctx["va"][blk].rearrange("p (h d) -> p h d", h=8), vaug[:], [ctx["name"] + "va"], "d_vaug", R=["vaug"])
            b.op("pe", lambda e: e.matmul(F[4][:, 16:24], lhsT=ut, rhs=lf_ap, start=True, stop=True), R=["cst", lf_key], W=["F4"], inc=False)
            b.op("pe", lambda e: e.matmul(F[4][:, 24:32], lhsT=onesf[:], rhs=lf_ap, start=True, stop=True), R=["onesf", lf_key], W=["F4"])
            nc_blk = ctx["negc"](blk)
            b.op("dve", lambda e: e.scalar_tensor_tensor(out=nc_blk, in0=F[4][:, 16:24], scalar=-1.0, in1=ncarry[:], op0=ALU.mult, op1=ALU.add), R=["F4", "ncarry"], W=[ctx["negk"]])
            b.op("dve", lambda e: e.scalar_tensor_tensor(out=ncarry[:], in0=F[4][:, 24:32], scalar=-1.0, in1=ncarry[:], op0=ALU.mult, op1=ALU.add), R=["F4", "ncarry"], W=["ncarry"])

        def mix_tile(ctx, kind, x_src, blk, qi, vcol, first, out_rows):
            full = kind != "P"
            s = tile_no[0] % 2
            ps_ = 1 - s
            tile_no[0] += 1
            xk, hk = "xt%d" % s, "hT%d" % s
            if kind == "S":
                ld("sp", xt[s][0:4, :], x_src, [xk], "d_" + xk)
            else:
                ld("sp", xt[s][:], x_src, [xk], "d_" + xk)
            rms_rstd(xt[s][:], 1024, xn[:], ms, xk, "xn", "ms")
            b.op("act", lambda e: e.activation(out=xn[:], in_=xt[s][:], func=AF.Copy, scale=ms[:, 0:1]), R=[xk, "ms"], W=["xn"])
            tv0 = T[0][:].rearrange("p (c t) -> p c t", c=8)
            transposes(tv0, "T0", xn, 8, "xn")
            b.op("dve", lambda e: e.tensor_copy(out=hT[s][:, :, 3:131], in_=tv0), R=["T0"], W=[hk])
            if first:
                b.op("dve", lambda e: e.memset(hT[s][:, :, 0:3], 0.0), W=[hk])
            else:
                b.op("dve", lambda e: e.tensor_copy(out=hT[s][:, :, 0:3], in_=hT[ps_][:, :, 128:131]), R=["hT%d" % ps_], W=[hk])
            fb = [0]

            def tm(c0, c1):
                i = fb[0] % 2
                fb[0] += 1
                mm_group(F[i][:, 0:c1 - c0], FK[i], [(hT[s][:, c, 3:131], winb[:, c, c0:c1]) for c in range(8)], [hk, "winb"])
                return F[i], FK[i]

            def tmconv(blk2, want_raw):
                wdt = 256 if (kind == "P" and blk2 == 1 and blk != NPRE - 1) else 512
                c0 = 2056 + blk2 * 512
                pr, prk = tm(c0, c0 + wdt)
                cs = slice(blk2 * 512, blk2 * 512 + wdt)
                for j in range(4):
                    b.op("dve", lambda e, j=j: e.tensor_tensor(out=cur[s][:, j, cs], in0=pr[:, 0:wdt], in1=wbc[:, j, cs], op=ALU.mult), R=[prk, "wbc"], W=["cur%d" % s])
                pairs = [(identb[:] if j == 3 else shm[:, j, :], cur[s][:, j, cs]) for j in range(4)]
                Rk = ["identb", "shm", "cur%d" % s]
                if kind == "S":
                    pairs += [(sh9[0:9, :], scw[0:9, cs])]
                    Rk += ["sh9", "scw"]
                elif not first:
                    pairs += [(shm[:, 3 + j, :], cur[ps_][:, j, cs]) for j in range(3)]
                    Rk += ["cur%d" % ps_]
                mm_group(F[2 + blk2][:, 0:wdt], FK[2 + blk2], pairs, Rk)
                return F[2 + blk2], FK[2 + blk2], wdt

            if full:
                pq, pqk = tm(0, 512)
                b.op("act", lambda e: e.activation(out=qb[:], in_=pq[:, :], func=AF.Copy, scale=ATT_SCALE), R=[pqk], W=["qb"])
            pk, pkk = tm(512, 1024)
            b.op("dve", lambda e: e.tensor_copy(out=kf[:], in_=pk[:, :]), R=[pkk], W=["kf"])
            if full:
                ld("pool", out_rows["k"], kf[0:out_rows["n"], :], [], "d_kf", R=["kf"])
            pv, pvk = tm(1024, 1536)
            if full:
                b.op("act", lambda e: e.copy(out=vf[:], in_=pv[:, :]), R=[pvk], W=["vf"])
                ld("pool", out_rows["v"], vf[0:out_rows["n"], :], [], "d_vf", R=["vf"])
            mm_group(F[4][:, 0:8], "F4", [(hT[s][:, c, 3:131], winb[:, c, 1536:1544]) for c in range(8)], [hk, "winb"])
            mm_group(F[4][:, 8:16], "F4", [(hT[s][:, c, 3:131], winb[:, c, 3080:3088]) for c in range(8)], [hk, "winb"])
            b.op("dve", lambda e: e.tensor_tensor(out=t16[:], in0=F[4][:, 0:16], in1=rvs[:, 0:16], op=ALU.add), R=["F4", "rvs"], W=["t16"])
            b.op("act", lambda e: e.activation(out=e16[:, 0:8], in_=t16[:, 0:8], func=AF.Exp, scale=-1.0), R=["t16"], W=["e16"])
            b.op("act", lambda e: e.activation(out=e16[:, 8:16], in_=t16[:, 8:16], func=AF.Exp), R=["t16"], W=["e16"])
            b.op("act", lambda e: e.activation(out=e16[:], in_=e16[:], func=AF.Ln, bias=1.0), R=["e16"], W=["e16"])
            if vcol is None:
                b.op("dve", lambda e: e.tensor_scalar(out=lf[:], in0=e16[:, 0:8], scalar1=-1.0, scalar2=None, op0=ALU.mult), R=["e16"], W=["lf"])
                b.op("dve", lambda e: e.tensor_copy(out=dtv[:], in_=e16[:, 8:16]), R=["e16"], W=["dtv"])
            else:
                b.op("dve", lambda e: e.tensor_scalar(out=lf[:], in0=e16[:, 0:8], scalar1=-1.0, scalar2=vcol, op0=ALU.mult, op1=ALU.mult), R=["e16", "validt"], W=["lf"])
                b.op("dve", lambda e: e.tensor_scalar(out=dtv[:], in0=e16[:, 8:16], scalar1=vcol, scalar2=None, op0=ALU.mult), R=["e16", "validt"], W=["dtv"])
            if full:
                ld("pool", out_rows["lf"], lf[0:out_rows["n"], :], [], "d_lf", R=["lf"])
            kv_post(ctx, blk, kf[:], "kf", pv[:, :], pvk, lf[:], "lf", vcol)
            if full:
                b.op("dve", lambda e: e.tensor_tensor(out=qk, in0=qb[:], in1=kb[:], op=ALU.mult), R=["qb", "kb"], W=["Rm"])
                b.op("dve", lambda e: e.tensor_reduce(out=mdiag[:], in_=qk.rearrange("p (h d) -> p h d", h=8), axis=AX.X, op=ALU.add), R=["Rm"], W=["mdiag"])
                nc_blk = ctx["negc"](blk)
                b.op("dve", lambda e: e.scalar_tensor_tensor(out=cm[:], in0=nc_blk, scalar=-1.0, in1=mdiag[:], op0=ALU.mult, op1=ALU.subtract), R=[ctx["negk"], "mdiag"], W=["cm"])
                b.op("dve", lambda e: e.tensor_copy(out=aug[:, 0:8], in_=cm[:]), R=["cm"], W=["aug"])
                b.op("dve", lambda e: e.tensor_tensor(out=r1[:], in0=cm[:], in1=aug[:, 0:8], op=ALU.subtract), R=["cm", "aug"], W=["r1"])
                b.op("dve", lambda e: e.tensor_copy(out=aug[:, 8:16], in_=r1[:]), R=["r1"], W=["aug"])
                b.op("dve", lambda e: e.tensor_tensor(out=r1[:], in0=r1[:], in1=aug[:, 8:16], op=ALU.subtract), R=["r1", "aug"], W=["r1"])
                b.op("dve", lambda e: e.tensor_copy(out=aug[:, 16:24], in_=r1[:]), R=["r1"], W=["aug"])
                b.op("pe", lambda e: e.transpose(out=T[1][:, 512:640], in_=aug[:, :], identity=identb[:]), R=["aug", "identb"], W=["T1"])
                b.op("dve", lambda e: e.tensor_copy(out=augT[:], in_=T[1][:, 512:640]), R=["T1"], W=["augT"])
                tv = T[1][:, 0:512].rearrange("p (c t) -> p c t", c=4)
                if kind == "S":
                    b.op("dve", lambda e: e.tensor_tensor(out=A24[0:24, :].rearrange("p (h q) -> p h q", h=8), in0=augT[0:24, 0:4].unsqueeze(1).broadcast_to([24, 8, 4]),
                                                          in1=cst[0:24, 520:528].unsqueeze(2).broadcast_to([24, 8, 4]), op=ALU.mult), R=["augT", "cst"], W=["A24"])
                    transposes(tv, "T1", qb, 4, "qb")
                    b.op("dve", lambda e: e.tensor_copy(out=Qbd[0:64, :, 0:4], in_=tv[0:64, :, 0:4]), R=["T1"], W=["Qbd"])
                    b.op("dve", lambda e: e.tensor_copy(out=Qbd[64:128, :, 4:8], in_=tv[64:128, :, 0:4]), R=["T1"], W=["Qbd"])
                else:
                    ld("pool", ctx["qa"][:, qi * P:(qi + 1) * P], augT[0:24, :], [ctx["name"] + "qa"], "d_augT", R=["augT"])
                    transposes(tv, "T1", qb, 4, "qb")
                    b.op("dve", lambda e: e.tensor_copy(out=trs[:], in_=tv), R=["T1"], W=["trs"])
                    ld("pool", ctx["qT"].rearrange("(c p) t -> p c t", p=P)[:, :, qi * P:(qi + 1) * P], trs[:], [ctx["name"] + "qT"], "d_trsq", R=["trs"])
                pz, pzk = tm(1544, 2056)
                b.op("act", lambda e: e.activation(out=sz[:], in_=pz[:, :], func=AF.Silu), R=[pzk], W=["sz"])
            want_raw = full and out_rows.get("conv") is not None
            if want_raw:
                for hb in range(2):
                    prr, prrk = tm(2056 + hb * 512, 2056 + (hb + 1) * 512)
                    b.op("act", lambda e, hb=hb: e.copy(out=rawx[:, hb * 512:(hb + 1) * 512], in_=prr[:, :]), R=[prrk], W=["Rm"])
                r0, r1_ = out_rows["convrows"]
                ld("pool", out_rows["conv"], rawx[r0:r1_, :], [], "d_rawx", R=["Rm"])
            px, pxk, _ = tmconv(0, want_raw)
            b.op("dve", lambda e: e.tensor_tensor(out=pre[:], in0=px[:, :], in1=convb[:, 0:512], op=ALU.add), R=[pxk, "convb"], W=["pre"])
            b.op("act", lambda e: e.activation(out=xact[:], in_=pre[:], func=AF.Silu), R=["pre"], W=["xact"])
            pbc, pbck, wd1 = tmconv(1, want_raw)
            b.op("dve", lambda e: e.tensor_tensor(out=pre[:, 0:wd1], in0=pbc[:, 0:wd1], in1=convb[:, 512:512 + wd1], op=ALU.add), R=[pbck, "convb"], W=["pre"])
            b.op("act", lambda e: e.activation(out=bcact[:, 0:wd1], in_=pre[:, 0:wd1], func=AF.Silu), R=["pre"], W=["bcact"])
            b.op("dve", lambda e: e.tensor_tensor(out=av[:], in0=dtv[:], in1=nexpA[:], op=ALU.mult), R=["dtv", "nexpA"], W=["av"])
            b.op("pe", lambda e: e.matmul(F[4][:, 32:40], lhsT=sl, rhs=av[:], start=True, stop=True), R=["cst", "av"], W=["F4"], inc=False)
            b.op("pe", lambda e: e.matmul(F[4][:, 40:48], lhsT=ut, rhs=av[:], start=True, stop=True), R=["cst", "av"], W=["F4"], inc=False)
            b.op("pe", lambda e: e.matmul(F[4][:, 48:56], lhsT=onesf[:], rhs=av[:], start=True, stop=True), R=["onesf", "av"], W=["F4"])
            b.op("act", lambda e: e.activation(out=E24[:], in_=F[4][:, 32:56], func=AF.Exp), R=["F4"], W=["E24"])
            xav = xact[:].rearrange("p (h d) -> p h d", h=8)
            b.op("dve", lambda e: e.tensor_tensor(out=xdt[:], in0=xav, in1=dtv[:].unsqueeze(2).broadcast_to([P, 8, 64]), op=ALU.mult), R=["xact", "dtv"], W=["xdt"])
            b.op("dve", lambda e: e.tensor_tensor(out=xdd[:], in0=xdt[:], in1=E24[:, 0:8].unsqueeze(2).broadcast_to([P, 8, 64]), op=ALU.mult), R=["xdt", "E24"], W=["xdd"])
            if full:
                tvb = T[1][:, 0:512].rearrange("p (c t) -> p c t", c=4)
                transposes(tvb, "T1", bcact, 4, "bcact")
                b.op("act", lambda e: e.copy(out=bcT[:], in_=tvb), R=["T1"], W=["bcT"])
                b.op("dve", lambda e: e.tensor_copy(out=Sb[:], in_=S[:]), R=["S"], W=["Sb"])
                for g in range(2):
                    b.op("pe", lambda e, g=g: e.matmul(F[1][:, g * 256:(g + 1) * 256], lhsT=bcT[:, 2 + g, :], rhs=Sb[:, g * 256:(g + 1) * 256], start=True, stop=True), R=["bcT", "Sb"], W=["F1"], inc=(g == 1))
                for g in range(2):
                    b.op("pe", lambda e, g=g: e.matmul(F[4][:, 256 + g * 128:256 + (g + 1) * 128], lhsT=bcT[:, g, :], rhs=bcT[:, 2 + g, :], start=True, stop=True), R=["bcT"], W=["F4"], inc=(g == 1))
                b.op("dve", lambda e: e.tensor_tensor(out=Rm[:], in0=ut.unsqueeze(1).broadcast_to([P, 8, P]), in1=av[:].unsqueeze(2).broadcast_to([P, 8, P]), op=ALU.mult), R=["cst", "av"], W=["Rm"])
                for hh in range(2):
                    b.op("pe", lambda e, hh=hh: e.matmul(F[2 + hh][:, :], lhsT=sl, rhs=Rm[:, hh * 4:(hh + 1) * 4, :].rearrange("p h t -> p (h t)"), start=True, stop=True), R=["cst", "Rm"], W=[FK[2 + hh]])
                    b.op("act", lambda e, hh=hh: e.activation(out=Dm[:, hh * 4:(hh + 1) * 4, :].rearrange("p h t -> p (h t)"), in_=F[2 + hh][:, :], func=AF.Exp), R=[FK[2 + hh]], W=["Dm"])
                b.op("dve", lambda e: e.tensor_tensor(out=GTm[:], in0=F[4][:, 256:512].rearrange("p (g t) -> p g t", g=2), in1=ut.unsqueeze(1).broadcast_to([P, 2, P]), op=ALU.mult), R=["F4", "cst"], W=["GTm"])
                for g in range(2):
                    b.op("dve", lambda e, g=g: e.tensor_tensor(out=Mm[:, 4 * g:4 * g + 4, :], in0=Dm[:, 4 * g:4 * g + 4, :], in1=GTm[:, g, :].unsqueeze(1).broadcast_to([P, 4, P]), op=ALU.mult), R=["Dm", "GTm"], W=["Mm"])
                for h in range(8):
                    b.op("pe", lambda e, h=h: e.matmul(F[0][:, h * 64:(h + 1) * 64], lhsT=Mm[:, h, :], rhs=xdt[:, h, :], start=True, stop=True), R=["Mm", "xdt"], W=["F0"], inc=(h == 7))
                y3 = ysb[:].rearrange("p (h d) -> p h d", h=8)
                b.op("dve", lambda e: e.tensor_tensor(out=y3, in0=F[1][:, :].rearrange("p (h d) -> p h d", h=8), in1=E24[:, 8:16].unsqueeze(2).broadcast_to([P, 8, 64]), op=ALU.mult), R=["F1", "E24"], W=["ysb"])
                b.op("dve", lambda e: e.tensor_tensor(out=ysb[:], in0=ysb[:], in1=F[0][:, :], op=ALU.add), R=["ysb", "F0"], W=["ysb"])
                b.op("dve", lambda e: e.tensor_tensor(out=ytmp.rearrange("p (h d) -> p h d", h=8), in0=xav, in1=rvs[:, 24:32].unsqueeze(2).broadcast_to([P, 8, 64]), op=ALU.mult), R=["xact", "rvs"], W=["Rm"])
                b.op("dve", lambda e: e.tensor_tensor(out=ysb[:], in0=ysb[:], in1=ytmp, op=ALU.add), R=["ysb", "Rm"], W=["ysb"])
                b.op("dve", lambda e: e.tensor_tensor(out=ysb[:], in0=ysb[:], in1=sz[:], op=ALU.mult), R=["ysb", "sz"], W=["ysb"])
                rms_rstd(ysb[:], 512, yn[:], ms[:, 1:2], "ysb", "yn", "ms")
                b.op("act", lambda e: e.activation(out=yn[:], in_=ysb[:], func=AF.Copy, scale=ms[:, 1:2]), R=["ysb", "ms"], W=["yn"])
                ld("pool", out_rows["mixed"], yn[:], [ctx["name"] + "mixed_ssm"], "d_yn", R=["yn"])
            for g in range(2):
                b.op("pe", lambda e, g=g: e.matmul(F[5][:, g * 256:(g + 1) * 256], lhsT=bcact[:, g * 128:(g + 1) * 128], rhs=xdd[:, 4 * g:4 * g + 4, :].rearrange("p h d -> p (h d)"), start=True, stop=True), R=["bcact", "xdd"], W=["F5"], inc=(g == 1))
            S3 = S[:].rearrange("p (h d) -> p h d", h=8)
            b.op("dve", lambda e: e.tensor_tensor(out=S3, in0=S3, in1=E24[:, 16:24].unsqueeze(2).broadcast_to([P, 8, 64]), op=ALU.mult), R=["S", "E24"], W=["S"])
            b.op("dve", lambda e: e.tensor_tensor(out=S[:], in0=S[:], in1=F[5][:, :], op=ALU.add), R=["S", "F5"], W=["S"])

        def state_out(dst):
            for c in range(4):
                b.op("pe", lambda e, c=c: e.transpose(out=F[5][:, c * P:(c + 1) * P], in_=S[:, c * P:(c + 1) * P], identity=identf), R=["S", "cst"], W=["F5"], inc=(c == 3))
            b.op("dve", lambda e: e.tensor_copy(out=sst[:].rearrange("p c n -> p (c n)"), in_=F[5][:, :]), R=["F5"], W=["sst"])
            ld("pool", dst.rearrange("(c p) n -> p c n", p=P), sst[:], [], "d_sst", R=["sst"])

        ctxp = {"name": "p", "kT": kT_p, "va": va_p, "qT": qT_p, "qa": qa_p, "negc": (lambda blk: negc_p[:, blk, :]), "negk": "negc_p"}
        b.op("dve", lambda e: e.memset(S[:], 0.0), W=["S"])
        b.op("dve", lambda e: e.memset(ncarry[:], 0.0), W=["ncarry"])
        for t in range(NPRE):
            mix_tile(ctxp, "P", xpre[t * P:(t + 1) * P, :], t, None, validt[:, t:t + 1], t == 0, None)
            if t + 1 == STOPT:
                finish(b)
        for t in range(NOWN):
            orow = {"n": P, "k": k_o[t * P:(t + 1) * P, :], "v": v_o[t * P:(t + 1) * P, :], "lf": lf_o[t * P:(t + 1) * P, :],
                    "mixed": mixed_p[t * P:(t + 1) * P, 512:1024]}
            if t == NOWN - 1:
                orow["conv"] = conv_o
                orow["convrows"] = (125, 128)
            mix_tile(ctxp, "O", xo[t * P:(t + 1) * P, :], NPRE + t, t, None, False, orow)
            if t + 1 == STOPO:
                finish(b)
        state_out(ssm_o)
        b.barrier()
        if STOP == 1:
            finish(b)
        for t_ in (xt[0], xt[1]):
            b.op("dve", lambda e, t_=t_: e.memset(t_[:], 0.0), W=["xt0", "xt1"])
        blkc = [0]

        def s_block(kT_ap, kT_key, vb_ap, vb_key, rk_ap, rk_key, mask, first, last):
            n = blkc[0] % 2
            blkc[0] += 1
            sc, sck = F[n][:, 0:32], FK[n]
            b.op("pe", lambda e: e.matmul(sc, lhsT=onesb[0:24, :], rhs=A24[0:24, :], start=True, stop=False), R=["init", "A24"], W=[sck], inc=False)
            for c in range(4):
                b.op("pe", lambda e, c=c: e.matmul(sc[:, 8 * c:8 * c + 8], lhsT=kT_ap[:, c, :], rhs=Qbd[:, c, :], start=False, stop=(c == 3 and not mask)),
                     R=[kT_key, "Qbd"], W=[sck], inc=(c == 3 and not mask))
            if mask:
                b.op("pe", lambda e: e.matmul(sc, lhsT=identb[:], rhs=negm32[:], start=False, stop=True), R=["identb", "init"], W=[sck])
            b.op("dve", lambda e: e.tensor_tensor(out=sb32[:].rearrange("p (h q) -> p h q", h=8), in0=sc.rearrange("p (h q) -> p h q", h=8),
                                                  in1=rk_ap.unsqueeze(2).broadcast_to([P, 8, 4]), op=ALU.add), R=[sck, rk_key], W=["sb32"])
            b.op("act", lambda e: e.activation(out=Pp[n][:], in_=sb32[:], func=AF.Exp), R=["sb32"], W=["Pp%d" % n])
            for hh in range(2):
                b.op("pe", lambda e, hh=hh: e.matmul(F[2 + hh][0:32, 0:260], lhsT=Pp[n][:, 0:32], rhs=vb_ap[:, 4 * hh:4 * hh + 4, :].rearrange("p h d -> p (h d)"),
                                                    start=first, stop=last), R=["Pp%d" % n, vb_key], W=[FK[2 + hh]])

        order = [(i, j) for i in range(NSEQ) for j in reversed(range(NPG))]
        gi = [0]

        def ensure_gather(n):
            while gi[0] <= n and gi[0] < len(order):
                m_ = gi[0]
                g_ = order[m_][0] * NPG + order[m_][1]
                b.gather(pg[m_ % 2][:], ckvd, idx[:, g_:g_ + 1], R=["idx"], W=["pg%d" % (m_ % 2)], sem="g_pg%d" % (m_ % 2))
                gi[0] += 1

        def prep(n):
            sl_ = n % 2
            pk_ = "pg%d" % sl_
            b.op("act", lambda e: e.copy(out=kb[:], in_=pg[sl_][:, 0:512]), R=[pk_], W=["kb"])
            tv = T[1][:, 0:512].rearrange("p (c t) -> p c t", c=4)
            transposes(tv, "T1", kb, 4, "kb")
            b.op("dve", lambda e: e.tensor_copy(out=kTp[sl_][:], in_=tv), R=["T1"], W=["kTp%d" % sl_])
            b.op("pool", lambda e: e.tensor_copy(out=vbp[sl_][:, :, 0:64], in_=pg[sl_][:, 512:1024].rearrange("p (h d) -> p h d", h=8)), R=[pk_], W=["vbp%d" % sl_])
            b.op("pe", lambda e: e.matmul(F[4][:, 0:8], lhsT=sl, rhs=pg[sl_][:, 1024:1032], start=True, stop=True), R=["cst", pk_], W=["F4"], inc=False)
            b.op("pe", lambda e: e.matmul(F[4][:, 8:16], lhsT=onesf[:], rhs=pg[sl_][:, 1024:1032], start=True, stop=True), R=["onesf", pk_], W=["F4"])
            b.op("dve", lambda e: e.tensor_tensor(out=rkp[sl_][:], in0=F[4][:, 0:8], in1=pcarry[:], op=ALU.add), R=["F4", "pcarry"], W=["rkp%d" % sl_])
            b.op("dve", lambda e: e.tensor_tensor(out=pcarry[:], in0=F[4][:, 8:16], in1=pcarry[:], op=ALU.add), R=["F4", "pcarry"], W=["pcarry"])

        ensure_gather(0)
        for i in range(NSEQ):
            base = i * NPG
            ctx = {"name": "s%d" % i, "nostore": True, "negc": (lambda blk: rkn[:]), "negk": "rkn"}
            b.op("dve", lambda e: e.memset(ncarry[:], 0.0), W=["ncarry"])
            b.op("dve", lambda e: e.memset(pcarry[:], 0.0), W=["pcarry"])
            if PIPE:
                ensure_gather(base + 1)
                prep(base)
            ld("sp", sst[:], sssmd[i].rearrange("(c p) n -> p c n", p=P), ["sst"], "d_sstl")
            for c in range(4):
                b.op("pe", lambda e, c=c: e.transpose(out=F[5][:, c * P:(c + 1) * P], in_=sst[:, c, :], identity=identf), R=["sst", "cst"], W=["F5"], inc=(c == 3))
            b.op("dve", lambda e: e.tensor_copy(out=S[:], in_=F[5][:, :]), R=["F5"], W=["S"])
            for k in range(3):
                ld("sp", scf[3 * k:3 * k + 3, :], sconvd[i, k:k + 1, :].partition_broadcast(3), ["scf"], "d_scf%d" % k)
            b.op("dve", lambda e: e.tensor_tensor(out=scw[0:9, :], in0=scf[0:9, :], in1=cw9[0:9, :], op=ALU.mult), R=["scf", "cw9"], W=["scw"])
            orow = {"n": 4, "k": k_s[i * 4:(i + 1) * 4, :], "v": v_s[i * 4:(i + 1) * 4, :], "lf": lf_s[i * 4:(i + 1) * 4, :],
                    "mixed": mixed_s[i][:, 512:1024], "conv": conv_s[i], "convrows": (1, 4)}
            mix_tile(ctx, "S", xsd[i * 4:(i + 1) * 4, :], NPG, 0, cst[:, 513:514], True, orow)
            state_out(ssm_s[i])
            s_block(trs, "trs", vaug, "vaug", rkn[:], "rkn", True, True, False)
            for jj in range(NPG):
                n = base + jj
                if PIPE:
                    if jj + 1 < NPG:
                        ensure_gather(n + 2)
                        prep(n + 1)
                else:
                    ensure_gather(n + 1)
                    prep(n)
                sl_ = n % 2
                s_block(kTp[sl_], "kTp%d" % sl_, vbp[sl_], "vbp%d" % sl_, rkp[sl_][:], "rkp%d" % sl_, False, False, jj == NPG - 1)
            for hh in range(2):
                a3 = F[2 + hh][0:32, 0:260].rearrange("p (h d) -> p h d", h=4)
                b.op("dve", lambda e, hh=hh, a3=a3: e.reciprocal(out=rl8[0:32, 4 * hh:4 * hh + 4], in_=a3[:, :, 64]), R=[FK[2 + hh]], W=["rl8"])
                b.op("dve", lambda e, hh=hh, a3=a3: e.tensor_tensor(out=ot[0:32, 4 * hh:4 * hh + 4, :], in0=a3[:, :, 0:64],
                                                                   in1=rl8[0:32, 4 * hh:4 * hh + 4].unsqueeze(2).broadcast_to([32, 4, 64]), op=ALU.mult), R=[FK[2 + hh], "rl8"], W=["ot"])
            for h in range(8):
                ld("sp", mixed_s[i][0:4, h * 64:(h + 1) * 64], ot[4 * h:4 * h + 4, h, :], ["s%dmixed_att" % i], "d_ot%d" % h, R=["ot"])
        b.barrier()

    with contextlib.ExitStack() as s2:
        KA = [b.sb("KA%d" % i, [P, NB_P * P], BF16, s2) for i in range(2)]
        VA = [b.sb("VA%d" % i, [P, NB_P, 65], BF16, s2) for i in range(2)]
        QA = [b.sb("QA%d" % i, [P, NOWN * P], BF16, s2) for i in range(2)]
        Pt = [b.sb("Pt%d" % i, [P, 512], BF16, s2) for i in range(3)]
        rl = b.sb("rl", [P, 4], F32, s2)
        att = b.sb("att", [P, 4, 64], BF16, s2)
        for i in range(2):
            b.op("pool", lambda e, i=i: e.memset(KA[i][:], 0.0), W=["KA%d" % i])
            b.op("pool", lambda e, i=i: e.memset(QA[i][:], 0.0), W=["QA%d" % i])
            b.op("pool", lambda e, i=i: e.memset(KA[i][64:67, :], 1.0), W=["KA%d" % i])
        hc = [0]
        pc = [0]

        def attend(ctx, nblk, nq_tiles, nqc, negc_fn, negk, mixed_fn):
            npre = nblk - nq_tiles
            for h in range(8):
                s = hc[0] % 2
                hc[0] += 1
                ka, va, qa = KA[s], VA[s], QA[s]
                kk, vk, qkx = "KA%d" % s, "VA%d" % s, "QA%d" % s
                ld("sp", ka[0:64, 0:nblk * P], ctx["kT"][h * 64:(h + 1) * 64, :], [kk], "d_" + kk, R=[ctx["name"] + "kT"])
                vsrc_all = ctx["va"].rearrange("b p (h d) -> p b h d", h=8)
                for b0 in range(0, nblk, 16):
                    b1 = min(nblk, b0 + 16)
                    ld("sp", va[:, b0:b1, :], vsrc_all[:, b0:b1, h, :], [vk], "d_%s_%d" % (vk, (b0 // 16) % 4), R=[ctx["name"] + "va"])
                ld("sp", qa[0:64, 0:nq_tiles * P], ctx["qT"][h * 64:(h + 1) * 64, :], [qkx], "d_" + qkx, R=[ctx["name"] + "qT"])
                ld("sp", qa[64:67, 0:nq_tiles * P], ctx["qa"].rearrange("(j h) t -> h j t", h=8)[h], [qkx], "d_" + qkx + "b", R=[ctx["name"] + "qa"])
                for G in range(0, nq_tiles, 4):
                    ng = min(4, nq_tiles - G)
                    W_ = (ng - 1) * P + nqc
                    last_j = npre + G + ng - 1
                    def qk_step(j, slot):
                        r = j - (npre + G)
                        c0 = 0 if r < 0 else r * P
                        sc, sck = F[slot % 2], FK[slot % 2]
                        diag = r >= 0
                        b.op("pe", lambda e: e.matmul(sc[:, c0:W_], lhsT=ka[0:67, j * P:(j + 1) * P], rhs=qa[0:67, G * P + c0:G * P + W_], start=True, stop=not diag),
                             R=[kk, qkx], W=[sck], inc=not diag)
                        if diag:
                            wd_ = min(P, W_ - c0)
                            b.op("pe", lambda e: e.matmul(sc[:, c0:c0 + wd_], lhsT=identb[:], rhs=negmb[:, 0:wd_], start=False, stop=True),
                                 R=["identb", "negmb"], W=[sck])

                    def ex_pv_step(j, slot):
                        r = j - (npre + G)
                        c0 = 0 if r < 0 else r * P
                        sc, sck = F[slot % 2], FK[slot % 2]
                        pt_, ptk = Pt[slot % 3], "Pt%d" % (slot % 3)
                        b.op("act", lambda e: e.activation(out=pt_[:, c0:W_], in_=sc[:, c0:W_], func=AF.Exp, bias=negc_fn(j)[:, h:h + 1]),
                             R=[sck, negk], W=[ptk])
                        for qbk in range(max(r, 0), ng):
                            w_ = nqc if qbk == ng - 1 else P
                            w_ = min(w_, W_ - qbk * P)
                            b.op("pe", lambda e, qbk=qbk, w_=w_: e.matmul(F[2 + qbk][0:w_, 0:65], lhsT=pt_[:, qbk * P:qbk * P + w_], rhs=va[:, j, :],
                                                                   start=(j == 0), stop=(j == npre + G + qbk)),
                                 R=[ptk, vk], W=[FK[2 + qbk]], inc=True)

                    nsteps = last_j + 1
                    qk_step(0, pc[0])
                    for j in range(nsteps):
                        if j + 1 < nsteps:
                            qk_step(j + 1, pc[0] + j + 1)
                        ex_pv_step(j, pc[0] + j)
                    pc[0] += nsteps
                    for qbk in range(ng):
                        b.op("dve", lambda e, qbk=qbk: e.reciprocal(out=rl[:, qbk:qbk + 1], in_=F[2 + qbk][:, 64:65]), R=[FK[2 + qbk]], W=["rl"])
                        b.op("dve", lambda e, qbk=qbk: e.tensor_scalar(out=att[:, qbk, :], in0=F[2 + qbk][:, 0:64], scalar1=rl[:, qbk:qbk + 1], scalar2=None, op0=ALU.mult), R=[FK[2 + qbk], "rl"], W=["att"])
                    ld("pool", mixed_fn(G, ng, h), att[:, 0:ng, :], [ctx["name"] + "mixed_att"], "d_att", R=["att"])

        for i_ in range(2, 6):
            b.op("dve", lambda e, i_=i_: e.memset(F[i_][:, :], 1.0), W=[FK[i_]])
        attend(ctxp, NB_P, NOWN, P, lambda j: negc_p[:, j, :], "negc_p",
               lambda G, ng, h: mixed_p[G * P:(G + ng) * P, h * 64:(h + 1) * 64].rearrange("(q p) d -> p q d", p=P))
        b.barrier()

    with contextlib.ExitStack() as s3:
        woutb = b.sb("woutb", [P, 8, 1024], BF16, s3)
        wcqb = b.sb("wcqb", [P, 8, 1024], BF16, s3)
        wcob = b.sb("wcob", [P, 8, 1024], BF16, s3)
        wckb = b.sb("wckb", [P, 8, 1024], BF16, s3)
        wcvb = b.sb("wcvb", [P, 8, 1024], BF16, s3)
        gout = b.sb("gout", [P, 8], F32, s3)
        b.op("dve", lambda e: e.memset(gout[:], 1.0), W=["gcol2"])
        b.op("dve", lambda e: e.tensor_copy(out=gout[:, 4:8], in_=gcol[:, 32:36]), R=["gcol"], W=["gcol2"])
        with contextlib.ExitStack() as s3a:
            stage = [b.sb("stage%d" % i, [P, 8 * 512], F32, s3a) for i in range(2)]
            for (dst, wd, gc, key) in ((woutb, w_out, gout[:, 0:8], "woutb"), (wcqb, w_cq, gcol[:, 8:16], "wcqb"), (wcob, w_co, None, "wcob"),
                                       (wckb, w_ck, gcol[:, 16:24], "wckb"), (wcvb, w_cv, gcol[:, 16:24], "wcvb")):
                for c0 in (0, 512):
                    load_weight(stage, dst, wd, 8, c0, c0 + 512, gc=gc, dst_off=c0, key=key)
            b.barrier()
        xt = [b.sb("xt%d" % i, [P, 1024], F32, s3) for i in range(2)]
        mxt = [b.sb("mxt%d" % i, [P, 1024], BF16, s3) for i in range(2)]
        junk = b.sb("junk", [P, 1024], F32, s3)
        ms = b.sb("ms", [P, 4], F32, s3)
        xn = b.sb("xn", [P, 1024], BF16, s3)
        hT = b.sb("hT", [P, 8, P], BF16, s3)
        x1 = b.sb("x1", [P, 1024], F32, s3)
        q2T = b.sb("q2T", [P, 8, P], BF16, s3)
        mkT = [b.sb("mkT%d" % i, [P, 8, 256], BF16, s3) for i in range(2)]
        mvb = [b.sb("mvb%d" % i, [P, 2, 1024], BF16, s3) for i in range(2)]
        mkb = b.sb("mkb", [P, 1024], BF16, s3)
        smax = b.sb("smax", [P, 4], F32, s3)
        ssum = b.sb("ssum", [P, 4], F32, s3)
        pexp = b.sb("pexp", [P, 4, 256], BF16, s3)
        pT = b.sb("pT", [P, 8, P], BF16, s3)
        ob = b.sb("ob", [P, 1024], BF16, s3)
        x2 = b.sb("x2", [P, 1024], F32, s3)
        for t_, k_ in ((xt[0], "xt0"), (xt[1], "xt1"), (mxt[0], "mxt0"), (mxt[1], "mxt1")):
            b.op("pool", lambda e, t_=t_: e.memset(t_[:], 0.0), W=[k_])

        def norm_T(src, skey, dstT, dkey):
            rms_rstd(src, 1024, junk[:], ms, skey, "junk", "ms")
            b.op("act", lambda e: e.activation(out=xn[:], in_=src, func=AF.Copy, scale=ms[:, 0:1]), R=[skey, "ms"], W=["xn"])
            tv0 = T[0][:].rearrange("p (c t) -> p c t", c=8)
            transposes(tv0, "T0", xn, 8, "xn")
            b.op("dve", lambda e: e.tensor_copy(out=dstT[:], in_=tv0), R=["T0"], W=[dkey])

        def build_mem(slot, ksrc_fn, vsrc_fn):
            for mt in range(2):
                kap, kkey = ksrc_fn(mt)
                b.op("act", lambda e: e.copy(out=mkb[:], in_=kap), R=[kkey], W=["mkb"])
                tv = T[1][:].rearrange("p (c t) -> p c t", c=8)
                transposes(tv, "T1", mkb, 8, "mkb")
                b.op("dve", lambda e, mt=mt: e.tensor_copy(out=mkT[slot][:, :, mt * P:(mt + 1) * P], in_=tv), R=["T1"], W=["mkT%d" % slot])
                vap, vkey = vsrc_fn(mt)
                b.op("dve", lambda e, mt=mt: e.tensor_copy(out=mvb[slot][:, mt, :], in_=vap), R=[vkey], W=["mvb%d" % slot])

        memk_sb = [b.sb("memk_sb%d" % i, [P, 1024], F32, s3) for i in range(2)]
        memv_sb = [b.sb("memv_sb%d" % i, [P, 1024], F32, s3) for i in range(2)]
        for mt in range(2):
            ld("sp", xt[0][:], memd[mt * P:(mt + 1) * P, :], ["xt0"], "d_xt0")
            norm_T(xt[0][:], "xt0", hT, "hT")
            for (wb, wk, dst, dk, od) in ((wckb, "wckb", memk_sb, "memk_sb%d" % mt, memk_o), (wcvb, "wcvb", memv_sb, "memv_sb%d" % mt, memv_o)):
                for hb in range(2):
                    mm_group(F[hb][:, :], FK[hb], [(hT[:, c, :], wb[:, c, hb * 512:(hb + 1) * 512]) for c in range(8)], ["hT", wk])
                    b.op("act", lambda e, hb=hb, dst=dst: e.copy(out=dst[mt][:, hb * 512:(hb + 1) * 512], in_=F[hb][:, :]), R=[FK[hb]], W=[dk])
                ld("pool", od[mt * P:(mt + 1) * P, :], dst[mt][:], [], "d_" + dk, R=[dk])
        build_mem(0, lambda mt: (memk_sb[mt][:], "memk_sb%d" % mt), lambda mt: (memv_sb[mt][:], "memv_sb%d" % mt))

        tl = [0]

        def row_tile(x_src, x_rows, mixed_src, mslot, x2_dst, nrows):
            s = tl[0] % 2
            tl[0] += 1
            xk, mk_ = "xt%d" % s, "mxt%d" % s
            ld("sp", xt[s][0:x_rows, :], x_src, [xk], "d_" + xk)
            ld("sp", mxt[s][0:x_rows, :], mixed_src, [mk_], "d_" + mk_, R=["mixedsrc"])
            tv0 = T[0][:].rearrange("p (c t) -> p c t", c=8)
            transposes(tv0, "T0", mxt[s], 8, mk_)
            b.op("dve", lambda e: e.tensor_copy(out=hT[:], in_=tv0), R=["T0"], W=["hT"])
            for hb in range(2):
                mm_group(F[hb][:, :], FK[hb], [(hT[:, c, :], woutb[:, c, hb * 512:(hb + 1) * 512]) for c in range(8)], ["hT", "woutb"])
                b.op("dve", lambda e, hb=hb: e.tensor_tensor(out=x1[:, hb * 512:(hb + 1) * 512], in0=F[hb][:, :], in1=xt[s][:, hb * 512:(hb + 1) * 512], op=ALU.add), R=[FK[hb], xk], W=["x1"])
            norm_T(x1[:], "x1", hT, "hT")
            for half in range(2):
                for cc in range(4):
                    ec = half * 4 + cc
                    mm_group(F[2 + half][:, cc * P:(cc + 1) * P], FK[2 + half], [(wcqb[:, c, ec * P:(ec + 1) * P], hT[:, c, :]) for c in range(8)], ["hT", "wcqb"])
                b.op("act", lambda e, half=half: e.activation(out=q2T[:, half * 4:(half + 1) * 4, :].rearrange("p c t -> p (c t)"), in_=F[2 + half][:, :], func=AF.Copy, scale=X_SCALE), R=[FK[2 + half]], W=["q2T"])
            for hp in range(2):
                for hh in range(2):
                    h = hp * 2 + hh
                    mm_group(F[hp][:, hh * 256:(hh + 1) * 256], FK[hp], [(q2T[:, 2 * h + cc, :], mkT[mslot][:, 2 * h + cc, :]) for cc in range(2)], ["q2T", "mkT%d" % mslot])
                b.op("dve", lambda e, hp=hp: e.tensor_reduce(out=smax[:, hp * 2:hp * 2 + 2], in_=F[hp][:, :].rearrange("p (h m) -> p h m", h=2), axis=AX.X, op=ALU.max), R=[FK[hp]], W=["smax"])
            b.op("dve", lambda e: e.tensor_scalar(out=smax[:], in0=smax[:], scalar1=-1.0, scalar2=None, op0=ALU.mult), R=["smax"], W=["smax"])
            for h in range(4):
                b.op("act", lambda e, h=h: e.activation(out=pexp[:, h, :], in_=F[h // 2][:, (h % 2) * 256:(h % 2 + 1) * 256], func=AF.Exp, bias=smax[:, h:h + 1], accum_out=ssum[:, h:h + 1]),
                     R=[FK[h // 2], "smax"], W=["pexp", "ssum"])
            tv1 = T[1][:].rearrange("p (c t) -> p c t", c=8)
            for h in range(4):
                for mt in range(2):
                    b.op("pe", lambda e, h=h, mt=mt: e.transpose(out=tv1[:, h * 2 + mt, :], in_=pexp[:, h, mt * P:(mt + 1) * P], identity=identb[:]), R=["pexp", "identb"], W=["T1"], inc=(h == 3 and mt == 1))
            b.op("dve", lambda e: e.tensor_copy(out=pT[:], in_=tv1), R=["T1"], W=["pT"])
            b.op("dve", lambda e: e.reciprocal(out=ssum[:], in_=ssum[:]), R=["ssum"], W=["ssum"])
            for hp in range(2):
                for hh in range(2):
                    h = hp * 2 + hh
                    mm_group(F[2 + hp][:, hh * 256:(hh + 1) * 256], FK[2 + hp], [(pT[:, h * 2 + mt, :], mvb[mslot][:, mt, h * 256:(h + 1) * 256]) for mt in range(2)], ["pT", "mvb%d" % mslot])
                b.op("dve", lambda e, hp=hp: e.tensor_tensor(out=ob[:, hp * 512:(hp + 1) * 512].rearrange("p (h d) -> p h d", h=2), in0=F[2 + hp][:, :].rearrange("p (h d) -> p h d", h=2),
                                                          in1=ssum[:, hp * 2:hp * 2 + 2].unsqueeze(2).broadcast_to([P, 2, 256]), op=ALU.mult), R=[FK[2 + hp], "ssum"], W=["ob"])
            tv0 = T[0][:].rearrange("p (c t) -> p c t", c=8)
            transposes(tv0, "T0", ob, 8, "ob")
            b.op("dve", lambda e: e.tensor_copy(out=hT[:], in_=tv0), R=["T0"], W=["hT"])
            for hb in range(2):
                mm_group(F[hb][:, :], FK[hb], [(hT[:, c, :], wcob[:, c, hb * 512:(hb + 1) * 512]) for c in range(8)], ["hT", "wcob"])
                b.op("dve", lambda e, hb=hb: e.tensor_tensor(out=x2[:, hb * 512:(hb + 1) * 512], in0=F[hb][:, :], in1=x1[:, hb * 512:(hb + 1) * 512], op=ALU.add), R=[FK[hb], "x1"], W=["x2"])
            ld("pool", x2_dst, x2[0:nrows, :], ["x2dst"], "d_x2", R=["x2"])

        for t in range(NOWN):
            row_tile(xo[t * P:(t + 1) * P, :], P, mixed_p[t * P:(t + 1) * P, :], 0, x2_p[t * P:(t + 1) * P, :], P)
        b.op("dve", lambda e: e.memset(xt[0][:], 0.0), W=["xt0"])
        b.op("dve", lambda e: e.memset(xt[1][:], 0.0), W=["xt1"])
        cmk_sb = [b.sb("cmk_sb%d" % i, [P, 1024], F32, s3) for i in range(2)]
        cmv_sb = [b.sb("cmv_sb%d" % i, [P, 1024], F32, s3) for i in range(2)]
        for i in range(NSEQ):
            for mt in range(2):
                ld("sp", cmk_sb[mt][:], cmkd[i, mt * P:(mt + 1) * P, :], ["cmk_sb%d" % mt], "d_cmk%d" % mt)
                ld("sp", cmv_sb[mt][:], cmvd[i, mt * P:(mt + 1) * P, :], ["cmv_sb%d" % mt], "d_cmv%d" % mt)
            build_mem(1, lambda mt: (cmk_sb[mt][:], "cmk_sb%d" % mt), lambda mt: (cmv_sb[mt][:], "cmv_sb%d" % mt))
            row_tile(xsd[i * 4:(i + 1) * 4, :], 4, mixed_s[i][0:4, :], 1, x2_s[i * 4:(i + 1) * 4, :], 4)
        b.barrier()

    with contextlib.ExitStack() as s4:
        wgb = b.sb("wgb", [P, 8, DFF], BF16, s4)
        wub = b.sb("wub", [P, 8, DFF], BF16, s4)
        wdb = b.sb("wdb", [P, 22, 1024], BF16, s4)
        with contextlib.ExitStack() as s4a:
            stage = [b.sb("stage%d" % i, [P, 22 * 256], F32, s4a) for i in range(2)]
            for (dst, wd, key) in ((wgb, w_gate, "wgb"), (wub, w_up, "wub")):
                for c0 in range(0, DFF, 512):
                    c1 = min(c0 + 512, DFF)
                    load_weight(stage, dst, wd, 8, c0, c1, gc=gcol[:, 24:32], dst_off=c0, key=key)
            for c0 in range(0, 1024, 256):
                load_weight(stage, wdb, w_down, 22, c0, c0 + 256, gc=None, dst_off=c0, key="wdb")
            b.barrier()
        xt = [b.sb("xt%d" % i, [P, 1024], F32, s4) for i in range(2)]
        junk = b.sb("junk", [P, 1024], F32, s4)
        ms = b.sb("ms", [P, 4], F32, s4)
        xn = b.sb("xn", [P, 1024], BF16, s4)
        hT = b.sb("hT", [P, 8, P], BF16, s4)
        gs = b.sb("gs", [P, 512], F32, s4)
        ub = b.sb("ub", [P, DFF], BF16, s4)
        uT = b.sb("uT", [P, 22, P], BF16, s4)
        x3 = b.sb("x3", [P, 1024], F32, s4)
        yo = [b.sb("yo%d" % i, [P, 1024], F32, s4) for i in range(2)]
        gfin = b.sb("gfin", [P, 1024], F32, s4)
        ld("sp", gfin[:], gfind.partition_broadcast(P), ["gfin"], "d_gfin")
        b.op("pool", lambda e: e.memset(xt[0][:], 0.0), W=["xt0"])
        b.op("pool", lambda e: e.memset(xt[1][:], 0.0), W=["xt1"])
        tl4 = [0]

        def ffn_tile(x_src, nrows, y_dst):
            s = tl4[0] % 2
            tl4[0] += 1
            xk = "xt%d" % s
            ld("sp", xt[s][0:nrows, :], x_src, [xk], "d_" + xk, R=["x2dst"])
            rms_rstd(xt[s][:], 1024, junk[:], ms, xk, "junk", "ms")
            b.op("act", lambda e: e.activation(out=xn[:], in_=xt[s][:], func=AF.Copy, scale=ms[:, 0:1]), R=[xk, "ms"], W=["xn"])
            tv0 = T[0][:].rearrange("p (c t) -> p c t", c=8)
            transposes(tv0, "T0", xn, 8, "xn")
            b.op("dve", lambda e: e.tensor_copy(out=hT[:], in_=tv0), R=["T0"], W=["hT"])
            nb = (DFF + 511) // 512
            for bi in range(nb):
                c0 = bi * 512
                c1 = min(c0 + 512, DFF)
                w = c1 - c0
                mm_group(F[0][:, 0:w], "F0", [(hT[:, c, :], wgb[:, c, c0:c1]) for c in range(8)], ["hT", "wgb"])
                mm_group(F[1][:, 0:w], "F1", [(hT[:, c, :], wub[:, c, c0:c1]) for c in range(8)], ["hT", "wub"])
                b.op("act", lambda e, w=w: e.activation(out=gs[:, 0:w], in_=F[0][:, 0:w], func=AF.Silu), R=["F0"], W=["gs"])
                b.op("dve", lambda e, w=w, c0=c0, c1=c1: e.tensor_tensor(out=ub[:, c0:c1], in0=gs[:, 0:w], in1=F[1][:, 0:w], op=ALU.mult), R=["gs", "F1"], W=["ub"])
            for t8 in range(0, 22, 8):
                n8 = min(8, 22 - t8)
                tv = T[(t8 // 8) % 2][:].rearrange("p (c t) -> p c t", c=8)
                tk_ = TK[(t8 // 8) % 2]
                for c in range(n8):
                    b.op("pe", lambda e, c=c, t8=t8, tv=tv: e.transpose(out=tv[:, c, :], in_=ub[:, (t8 + c) * P:(t8 + c + 1) * P], identity=identb[:]), R=["ub", "identb"], W=[tk_], inc=(c == n8 - 1))
                b.op("dve", lambda e, t8=t8, n8=n8, tv=tv: e.tensor_copy(out=uT[:, t8:t8 + n8, :], in_=tv[:, 0:n8, :]), R=[tk_], W=["uT"])
            for hb in range(2):
                mm_group(F[2 + hb][:, :], FK[2 + hb], [(uT[:, c, :], wdb[:, c, hb * 512:(hb + 1) * 512]) for c in range(22)], ["uT", "wdb"])
                b.op("dve", lambda e, hb=hb: e.tensor_tensor(out=x3[:, hb * 512:(hb + 1) * 512], in0=F[2 + hb][:, :], in1=xt[s][:, hb * 512:(hb + 1) * 512], op=ALU.add), R=[FK[2 + hb], xk], W=["x3"])
            rms_rstd(x3[:], 1024, junk[:], ms[:, 1:2], "x3", "junk", "ms")
            yk = "yo%d" % s
            b.op("dve", lambda e: e.scalar_tensor_tensor(out=yo[s][:], in0=x3[:], scalar=ms[:, 1:2], in1=gfin[:], op0=ALU.mult, op1=ALU.mult), R=["x3", "ms", "gfin"], W=[yk])
            ld("pool", y_dst, yo[s][0:nrows, :], [], "d_" + yk, R=[yk])

        for t in range(NOWN):
            ffn_tile(x2_p[t * P:(t + 1) * P, :], P, y_o[t * P:(t + 1) * P, :])
        ffn_tile(x2_s[0:NSEQ * 4, :], NSEQ * 4, y_s)
        b.barrier()

    b.es.close()
    return nc


def build(cfg=None):
    cfg = cfg or FULL_CFG
    b = Builder()
    try:
        return _build(b, cfg)
    except _Stop:
        try:
            b.es.close()
        except AssertionError:
            pass
        return b.nc


_NC_CACHE = {}


def make_shared(inp):
    f = np.float32
    P = 128
    cst = np.zeros((P, 1024), f)
    ii = np.arange(P)
    cst[:, 0:128] = np.eye(P, dtype=f)
    cst[:, 128:256] = (ii[:, None] <= ii[None, :]).astype(f)
    cst[:, 256:384] = (ii[:, None] > ii[None, :]).astype(f)
    cst[:, 384:512] = np.where(ii[:, None] > ii[None, :], -30000.0, 0.0).astype(f)
    cst[:, 512] = ii.astype(f)
    cst[0:4, 513] = 1.0
    for j in range(3):
        for h in range(8):
            cst[j * 8 + h, 520 + h] = 1.0
    for col in range(32):
        cst[:, 528 + col] = np.where(ii > (col % 4), -30000.0, 0.0)
    for k in range(3):
        for j in range(3):
            t = k - j
            if 0 <= t < 4:
                cst[3 * k + j, 896 + t] = 1.0

    cst2 = np.zeros((P, 6, P), f)
    for j in range(3):
        for t in range(P):
            s_ = t + j - 3
            if s_ >= 0:
                cst2[s_, j, t] = 1.0
            else:
                cst2[P + s_, 3 + j, t] = 1.0
    cst2 = cst2.reshape(P, 6 * P)

    def col8(g):
        return np.ascontiguousarray(np.asarray(g, f).reshape(-1, P).T)

    gcolv = np.zeros((P, 40), f)
    gcolv[:, 0:8] = col8(inp["norm_mix_g"][0])
    gcolv[:, 8:16] = col8(inp["norm_cross_g"][0])
    gcolv[:, 16:24] = col8(inp["norm_mem_g"][0])
    gcolv[:, 24:32] = col8(inp["norm_ffn_g"][0])
    gcolv[:, 32:36] = col8(inp["ssm_norm_g"][0])
    rvs = np.concatenate([np.asarray(inp[k][0], f) for k in ("b_forget", "dt_bias", "a_log", "d_skip")]).reshape(1, 32)
    nphys = np.asarray(inp["cache_k"]).shape[1]
    shared = {
        "ckv": np.concatenate([np.asarray(inp["cache_k"], f).reshape(nphys * 128, 512),
                               np.asarray(inp["cache_v"], f).reshape(nphys * 128, 512),
                               np.asarray(inp["cache_logf"], f).reshape(nphys * 128, 8)], axis=1),
        "gcol": gcolv, "rvs": rvs, "convb": np.asarray(inp["conv_b"][0], f).reshape(1, 1024),
        "convw": np.asarray(inp["conv_w"][0], f).reshape(1, 4096), "gfin": np.asarray(inp["final_norm_g"], f).reshape(1, 1024),
        "cst": cst, "cst2": cst2,
    }
    for k in ("w_in", "w_out", "w_cq", "w_ck", "w_cv", "w_co", "w_gate", "w_up", "w_down"):
        shared[k] = np.asarray(inp[k][0], f)
    return shared


def make_core_map(shared, inp, cfg, sq, own_start, s0):
    f = np.float32
    P = 128
    NPRE, NOWN, NSEQ, NPG = cfg["NPRE"], cfg["NOWN"], cfg["NSEQ"], cfg["NPG"]
    x_prompt = np.asarray(inp["x_prompt"], f)
    xpre = np.zeros((NPRE * P, 1024), f)
    npre_tok = min(own_start, NPRE * P)
    assert npre_tok == own_start
    if npre_tok:
        xpre[NPRE * P - npre_tok:] = x_prompt[sq, own_start - npre_tok:own_start]
    valid = np.zeros((P, NPRE), f)
    if npre_tok:
        valid[:, NPRE - npre_tok // P:] = 1.0
    sl_ = slice(s0, s0 + NSEQ)
    m = dict(shared)
    m.update({
        "xo": np.ascontiguousarray(x_prompt[sq, own_start:own_start + NOWN * P]),
        "xpre": xpre, "valid": valid,
        "xs": np.ascontiguousarray(np.asarray(inp["x_sample"], f)[sl_].reshape(NSEQ * 4, 1024)),
        "mem": np.ascontiguousarray(np.asarray(inp["mem_prompt"], f)[sq]),
        "pt": np.ascontiguousarray(np.asarray(inp["page_table"], np.int32)[sl_].reshape(1, NSEQ * NPG)),
        "cmk": np.ascontiguousarray(np.asarray(inp["cache_mem_k"], f)[0, sl_].reshape(NSEQ, 256, 1024)),
        "cmv": np.ascontiguousarray(np.asarray(inp["cache_mem_v"], f)[0, sl_].reshape(NSEQ, 256, 1024)),
        "sconv": np.ascontiguousarray(np.asarray(inp["state_conv"], f)[0, sl_]),
        "sssm": np.ascontiguousarray(np.asarray(inp["state_ssm"], f)[0, sl_].reshape(NSEQ, 512, 128)),
    })
    return m


def kernel(**inp):
    f = np.float32
    cfg = FULL_CFG
    NSEQ = cfg["NSEQ"]
    shared = make_shared(inp)
    in_maps = [make_core_map(shared, inp, cfg, c // 4, (c % 4) * 2048, c * NSEQ) for c in range(8)]
    if "nc" not in _NC_CACHE:
        _NC_CACHE["nc"] = build(cfg)
    res = run_bass_kernel_spmd(_NC_CACHE["nc"], in_maps, core_ids=list(range(8)))
    R = res.results

    def cat(name, shape):
        return np.concatenate([np.asarray(R[c][name], f) for c in range(8)], axis=0).reshape(shape)

    y_prompt = cat("y_o", (2, 8192, 1024))
    k_prompt = cat("k_o", (1, 2, 8192, 8, 64))
    v_prompt = cat("v_o", (1, 2, 8192, 8, 64))
    lf_prompt = cat("lf_o", (1, 2, 8192, 8))
    memk = np.stack([np.asarray(R[0]["memk_o"], f), np.asarray(R[4]["memk_o"], f)]).reshape(1, 2, 256, 4, 256)
    memv = np.stack([np.asarray(R[0]["memv_o"], f), np.asarray(R[4]["memv_o"], f)]).reshape(1, 2, 256, 4, 256)
    conv_p = np.stack([np.asarray(R[3]["conv_o"], f), np.asarray(R[7]["conv_o"], f)]).reshape(1, 2, 3, 1024)
    ssm_p = np.stack([np.asarray(R[3]["ssm_o"], f), np.asarray(R[7]["ssm_o"], f)]).reshape(1, 2, 8, 64, 128)
    y_sample = cat("y_s", (128, 4, 1024))
    k_sample = cat("k_s", (1, 128, 4, 8, 64))
    v_sample = cat("v_s", (1, 128, 4, 8, 64))
    lf_sample = cat("lf_s", (1, 128, 4, 8))
    conv_sm = cat("conv_s", (1, 128, 3, 1024))
    ssm_sm = cat("ssm_s", (1, 128, 8, 64, 128))
    return (y_prompt, y_sample, k_prompt, v_prompt, lf_prompt, memk, memv, conv_p, ssm_p,
            k_sample, v_sample, lf_sample, conv_sm, ssm_sm)
```
